# Optimizing a Trainium2 kernel written in Bass

```python
import math
import jax, jax.numpy as jnp
from jax import lax
import numpy as np

D_MODEL = 2048
BATCH = 8
SEQ = 2048
DEPTH = 1

MIX_WIDTH = D_MODEL
ATTN_WIDTH = MIX_WIDTH // 2
DELTA_WIDTH = MIX_WIDTH - ATTN_WIDTH

ATTN_HEAD_DIM = 64
N_ATTN_HEADS = ATTN_WIDTH // ATTN_HEAD_DIM
N_KV_HEADS = N_ATTN_HEADS // 4
GQA_GROUP = N_ATTN_HEADS // N_KV_HEADS
WINDOW = 128
ATTN_BLOCK = 128
NEG_INF = -1e30

N_BUCKETS = 32
MAX_DISTANCE = 128

DELTA_HEAD_DIM = 128
N_DELTA_HEADS = DELTA_WIDTH // DELTA_HEAD_DIM
CONV_WIDTH = 4
CHUNK = 64

D_FF = 4 * D_MODEL

DN_ALPHA = (2.0 * DEPTH) ** 0.25
DN_BETA = (8.0 * DEPTH) ** -0.25
LN_EPS = 1e-5
RMS_EPS = 1e-6

COL_ATTN_Q = N_ATTN_HEADS * ATTN_HEAD_DIM
COL_ATTN_KV = N_KV_HEADS * ATTN_HEAD_DIM
COL_DELTA_QKV = 3 * DELTA_WIDTH
COL_DELTA_SCALAR = N_DELTA_HEADS
COL_DELTA_Z = DELTA_WIDTH
N_IN_COLS = COL_ATTN_Q + 2 * COL_ATTN_KV + COL_DELTA_QKV + 2 * COL_DELTA_SCALAR + COL_DELTA_Z
SPLIT_POINTS = [int(s) for s in np.cumsum([COL_ATTN_Q, COL_ATTN_KV, COL_ATTN_KV, COL_DELTA_QKV,
                                              COL_DELTA_SCALAR, COL_DELTA_SCALAR])]

kernel_name = "hymba_swa_sink_gdn_deepnorm"


def layer_norm(x, g, b):
    xf = x.astype(jnp.float32)
    mu = jnp.mean(xf, axis=-1, keepdims=True)
    xc = xf - mu
    var = jnp.mean(xc * xc, axis=-1, keepdims=True)
    y = xc * lax.rsqrt(var + LN_EPS) * g.astype(jnp.float32) + b.astype(jnp.float32)
    return y.astype(x.dtype)


def t5_causal_bucket(dist):
    n = jnp.maximum(dist, 0)
    max_exact = N_BUCKETS // 2
    nf = jnp.maximum(n, 1).astype(jnp.float32)
    large = max_exact + (jnp.log(nf / max_exact) / math.log(MAX_DISTANCE / max_exact)
                         * (N_BUCKETS - max_exact)).astype(jnp.int32)
    large = jnp.minimum(large, N_BUCKETS - 1)
    return jnp.where(n < max_exact, n, large)


def sliding_window_attention(q, k, v, sinks, rel_bias):
    B, S = q.shape[0], q.shape[1]
    nb = S // ATTN_BLOCK
    qb = q.reshape(B, nb, ATTN_BLOCK, N_KV_HEADS, GQA_GROUP, ATTN_HEAD_DIM)
    kb = k.reshape(B, nb, ATTN_BLOCK, N_KV_HEADS, ATTN_HEAD_DIM)
    vb = v.reshape(B, nb, ATTN_BLOCK, N_KV_HEADS, ATTN_HEAD_DIM)
    pad = ((0, 0), (1, 0), (0, 0), (0, 0), (0, 0))
    kc = jnp.concatenate([jnp.pad(kb, pad)[:, :-1], kb], axis=2)
    vc = jnp.concatenate([jnp.pad(vb, pad)[:, :-1], vb], axis=2)
    logits = jnp.einsum('bnqhgd,bnkhd->bnhgqk', qb, kc).astype(jnp.float32) * (ATTN_HEAD_DIM ** -0.5)
    qi = jnp.arange(ATTN_BLOCK, dtype=jnp.int32)[:, None]
    kj = jnp.arange(2 * ATTN_BLOCK, dtype=jnp.int32)[None, :]
    dist = qi + ATTN_BLOCK - kj
    band = (dist >= 0) & (dist < WINDOW)
    blk = jnp.arange(nb, dtype=jnp.int32)[:, None, None]
    valid = band[None] & ~((blk == 0) & (kj[None] < ATTN_BLOCK))
    bias = rel_bias.astype(jnp.float32)[t5_causal_bucket(dist)]
    bias = bias.transpose(2, 0, 1).reshape(N_KV_HEADS, GQA_GROUP, ATTN_BLOCK, 2 * ATTN_BLOCK)
    logits = jnp.where(valid[None, :, None, None], logits + bias, NEG_INF)
    sink = jnp.broadcast_to(sinks.astype(jnp.float32).reshape(1, 1, N_KV_HEADS, GQA_GROUP, 1, 1),
                            logits.shape[:-1] + (1,))
    probs = jax.nn.softmax(jnp.concatenate([logits, sink], axis=-1), axis=-1)[..., :-1]
    out = jnp.einsum('bnhgqk,bnkhd->bnqhgd', probs.astype(v.dtype), vc)
    return out.reshape(B, S, N_ATTN_HEADS * ATTN_HEAD_DIM)


def chunked_gated_delta_rule(q, k, v, g, beta):
    B, S, H, dk = q.shape
    dv = v.shape[-1]
    nc = S // CHUNK

    def chunkify(t):
        return t.reshape(B, nc, CHUNK, H, -1).transpose(0, 3, 1, 2, 4)

    qc, kc, vc = chunkify(q), chunkify(k), chunkify(v)
    gc = g.reshape(B, nc, CHUNK, H).transpose(0, 3, 1, 2)
    bc = beta.reshape(B, nc, CHUNK, H).transpose(0, 3, 1, 2)
    G = jnp.cumsum(gc, axis=-1)
    tril = jnp.tril(jnp.ones((CHUNK, CHUNK), dtype=bool))
    strict = jnp.tril(jnp.ones((CHUNK, CHUNK), dtype=bool), -1)
    decay_mat = jnp.exp(jnp.where(tril, G[..., :, None] - G[..., None, :], -jnp.inf))
    kbeta = kc * bc[..., None]
    A = jnp.where(strict, jnp.einsum('bhnid,bhnjd->bhnij', kbeta, kc) * decay_mat, 0.0)
    eye = jnp.eye(CHUNK, dtype=jnp.float32)
    rhs = jnp.concatenate([vc * bc[..., None], kbeta * jnp.exp(G)[..., None]], axis=-1)
    sol = lax.linalg.triangular_solve(A + eye, rhs, left_side=True, lower=True, unit_diagonal=True)
    u, w = sol[..., :dv], sol[..., dv:]
    attn_intra = jnp.einsum('bhnid,bhnjd->bhnij', qc, kc) * decay_mat
    q_dec = qc * jnp.exp(G)[..., None]
    k_dec = kc * jnp.exp(G[..., -1:] - G)[..., None]
    g_last = jnp.exp(G[..., -1])

    def step(state, inp):
        q_d, k_d, u_c, w_c, a_c, gl = inp
        v_new = u_c - jnp.einsum('bhcd,bhde->bhce', w_c, state)
        o = jnp.einsum('bhcd,bhde->bhce', q_d, state) + jnp.einsum('bhij,bhje->bhie', a_c, v_new)
        state = state * gl[..., None, None] + jnp.einsum('bhcd,bhce->bhde', k_d, v_new)
        return state, o

    xs = (jnp.moveaxis(q_dec, 2, 0), jnp.moveaxis(k_dec, 2, 0), jnp.moveaxis(u, 2, 0),
          jnp.moveaxis(w, 2, 0), jnp.moveaxis(attn_intra, 2, 0), jnp.moveaxis(g_last, 2, 0))
    s0 = jnp.zeros((B, H, dk, dv), dtype=jnp.float32)
    _, o = lax.scan(step, s0, xs)
    return o.transpose(1, 0, 3, 2, 4).reshape(B, S, H, dv)


def l2_normalise(t):
    return t * lax.rsqrt(jnp.sum(t * t, axis=-1, keepdims=True) + RMS_EPS)


def hybrid_mixer(h, w_in, conv_w, a_log, dt_bias, delta_norm_w, sinks, rel_bias, w_o):
    B, S, _ = h.shape
    proj = h @ w_in
    q_a, k_a, v_a, qkv_d, a_raw, b_raw, z = jnp.split(proj, SPLIT_POINTS, axis=-1)

    attn_out = sliding_window_attention(
        q_a.reshape(B, S, N_ATTN_HEADS, ATTN_HEAD_DIM),
        k_a.reshape(B, S, N_KV_HEADS, ATTN_HEAD_DIM),
        v_a.reshape(B, S, N_KV_HEADS, ATTN_HEAD_DIM), sinks, rel_bias)

    qkv_d = lax.conv_general_dilated(qkv_d, conv_w, window_strides=(1,), padding=[(CONV_WIDTH - 1, 0)],
                                     dimension_numbers=('NWC', 'WIO', 'NWC'),
                                     feature_group_count=COL_DELTA_QKV)
    qkv_d = jax.nn.silu(qkv_d).astype(jnp.float32)
    q_d, k_d, v_d = jnp.split(qkv_d, 3, axis=-1)
    q_d = l2_normalise(q_d.reshape(B, S, N_DELTA_HEADS, DELTA_HEAD_DIM)) * (DELTA_HEAD_DIM ** -0.5)
    k_d = l2_normalise(k_d.reshape(B, S, N_DELTA_HEADS, DELTA_HEAD_DIM))
    v_d = v_d.reshape(B, S, N_DELTA_HEADS, DELTA_HEAD_DIM)
    g = -jnp.exp(a_log.astype(jnp.float32)) * jax.nn.softplus(a_raw.astype(jnp.float32) + dt_bias.astype(jnp.float32))
    beta = jax.nn.sigmoid(b_raw.astype(jnp.float32))
    o_d = chunked_gated_delta_rule(q_d, k_d, v_d, g, beta)
    o_d = o_d * lax.rsqrt(jnp.mean(o_d * o_d, axis=-1, keepdims=True) + RMS_EPS) * delta_norm_w.astype(jnp.float32)
    o_d = o_d * jax.nn.silu(z.astype(jnp.float32)).reshape(B, S, N_DELTA_HEADS, DELTA_HEAD_DIM)
    delta_out = o_d.reshape(B, S, DELTA_WIDTH).astype(h.dtype)

    mix = jnp.concatenate([attn_out, delta_out], axis=-1)
    return mix @ w_o


def squared_relu_mlp(h, w_up, w_down):
    a = jax.nn.relu(h @ w_up)
    return (a * a) @ w_down


def setup_inputs(seed: int = 0) -> dict:
    key = jax.random.key(seed)
    ks = jax.random.split(key, 16)
    f32 = jnp.float32
    x = jax.random.normal(ks[0], (BATCH, SEQ, D_MODEL), f32)
    w_in = jax.random.normal(ks[1], (DEPTH, D_MODEL, N_IN_COLS), f32) * D_MODEL ** -0.5
    conv_w = jax.random.normal(ks[2], (DEPTH, CONV_WIDTH, 1, COL_DELTA_QKV), f32) * CONV_WIDTH ** -0.5
    a_log = jnp.log(jax.random.uniform(ks[3], (DEPTH, N_DELTA_HEADS), f32, 1.0, 16.0))
    dt = jnp.exp(jax.random.uniform(ks[4], (DEPTH, N_DELTA_HEADS), f32, math.log(1e-3), math.log(1e-1)))
    dt_bias = dt + jnp.log(-jnp.expm1(-dt))
    delta_norm_w = 1.0 + 0.02 * jax.random.normal(ks[5], (DEPTH, DELTA_HEAD_DIM), f32)
    attn_sinks = 0.5 * jax.random.normal(ks[6], (DEPTH, N_ATTN_HEADS), f32)
    rel_bias = 0.5 * jax.random.normal(ks[7], (N_BUCKETS, N_ATTN_HEADS), f32)
    w_o = jax.random.normal(ks[8], (DEPTH, MIX_WIDTH, D_MODEL), f32) * (MIX_WIDTH ** -0.5 * DN_BETA)
    ln1_g = 1.0 + 0.02 * jax.random.normal(ks[9], (DEPTH, D_MODEL), f32)
    ln1_b = 0.02 * jax.random.normal(ks[10], (DEPTH, D_MODEL), f32)
    w_up = jax.random.normal(ks[11], (DEPTH, D_MODEL, D_FF), f32) * D_MODEL ** -0.5
    w_down = jax.random.normal(ks[12], (DEPTH, D_FF, D_MODEL), f32) * (D_FF ** -0.5 * DN_BETA)
    ln2_g = 1.0 + 0.02 * jax.random.normal(ks[13], (DEPTH, D_MODEL), f32)
    ln2_b = 0.02 * jax.random.normal(ks[14], (DEPTH, D_MODEL), f32)
    return {"x": x, "w_in": w_in, "conv_w": conv_w, "a_log": a_log, "dt_bias": dt_bias,
            "delta_norm_w": delta_norm_w, "attn_sinks": attn_sinks, "rel_bias": rel_bias,
            "w_o": w_o, "ln1_g": ln1_g, "ln1_b": ln1_b, "w_up": w_up, "w_down": w_down,
            "ln2_g": ln2_g, "ln2_b": ln2_b}


def reference(x, w_in, conv_w, a_log, dt_bias, delta_norm_w, attn_sinks, rel_bias,
              w_o, ln1_g, ln1_b, w_up, w_down, ln2_g, ln2_b):
    for l in range(DEPTH):
        mixed = hybrid_mixer(x, w_in[l], conv_w[l], a_log[l], dt_bias[l], delta_norm_w[l],
                             attn_sinks[l], rel_bias, w_o[l])
        x = layer_norm(DN_ALPHA * x + mixed, ln1_g[l], ln1_b[l])
        x = layer_norm(DN_ALPHA * x + squared_relu_mlp(x, w_up[l], w_down[l]), ln2_g[l], ln2_b[l])
    return x
```

```python
import math
import bisect
import numpy as np
import concourse.bass as bass
import concourse.mybir as mybir
from concourse.bass_utils import run_bass_kernel_spmd
from contextlib import ExitStack

F32 = mybir.dt.float32
BF16 = mybir.dt.bfloat16
ALU = mybir.AluOpType
AF = mybir.ActivationFunctionType

D = 2048
S_LEN = 2048
NCOL = 5648
DFF = 8192
ALPHA = 2.0 ** 0.25
LN_EPS = 1e-5
RMS_EPS = 1e-6
C_Q, C_K, C_V, C_QKVD, C_A, C_B, C_Z = 0, 1024, 1280, 1536, 4608, 4616, 4624


class Sched:
    ENG = ('pe', 'act', 'dve', 'pool', 'sp')

    def __init__(self):
        self.ops = []
        self.lw = {}
        self.rd = {}
        self.dkeys = []
        self.last_on = {}
        self.dma_since = []

    def add(self, eng, fn, reads=(), writes=(), dkey=None):
        i = len(self.ops)
        deps = set()
        for k in reads:
            w = self.lw.get(k)
            if w is not None:
                deps.add(w)
        for k in writes:
            w = self.lw.get(k)
            if w is not None:
                deps.add(w)
            r = self.rd.get(k)
            if r:
                deps.update(r.values())
        rk = ('dma', i) if dkey is not None else eng
        for k in reads:
            self.rd.setdefault(k, {})[rk] = i
        for k in writes:
            self.lw[k] = i
            self.rd[k] = {}
        deps.discard(i)
        if dkey is not None:
            if dkey not in self.dkeys:
                self.dkeys.append(dkey)
            self.dma_since.append(i)
        else:
            self.last_on[eng] = i
        self.ops.append(dict(eng=eng, fn=fn, deps=deps, dkey=dkey, sig=None))
        return i

    def barrier(self):
        deps = set(self.last_on.values()) | set(self.dma_since)
        self.dma_since = []
        for e in self.ENG:
            self.ops.append(dict(eng=e, fn=None, deps=set(deps), dkey=None, sig=None))

    def emit(self, nc, stack):
        ops = self.ops

        def skip(dop, op):
            return (dop['dkey'] is None and op['dkey'] is None and dop['eng'] == 'pe'
                    and op['eng'] == 'pe' and op['fn'] is not None)

        signaled = set()
        for op in ops:
            for d in op['deps']:
                if not skip(ops[d], op):
                    signaled.add(d)
        cnt = {e: 0 for e in self.ENG}
        dcnt = {k: 0 for k in self.dkeys}
        dhist = {k: [] for k in self.dkeys}
        for i, op in enumerate(ops):
            if op['dkey'] is not None:
                dcnt[op['dkey']] += 16
                op['sig'] = (('d', op['dkey']), dcnt[op['dkey']])
                dhist[op['dkey']].append(i)
            elif i in signaled:
                cnt[op['eng']] += 1
                op['sig'] = (('e', op['eng']), cnt[op['eng']])
        sems = {}
        for e in self.ENG:
            sems[('e', e)] = stack.enter_context(nc.semaphore("sem_" + e))
        for n, k in enumerate(self.dkeys):
            sems[('d', k)] = stack.enter_context(nc.semaphore("semd_%d" % n))
        block = stack.enter_context(nc.Block())
        per_eng = {e: [] for e in self.ENG}
        for i, op in enumerate(ops):
            per_eng[op['eng']].append(i)

        def run(eng_name, e):
            waited = {}
            for i in per_eng[eng_name]:
                op = ops[i]
                need = {}
                for d in op['deps']:
                    dop = ops[d]
                    if dop['sig'] is None or skip(dop, op):
                        continue
                    sk, c = dop['sig']
                    if dop['dkey'] is not None:
                        hl = dhist[dop['dkey']]
                        c = 16 * bisect.bisect_left(hl, i)
                    if c > need.get(sk, 0):
                        need[sk] = c
                for sk, c in need.items():
                    if waited.get(sk, 0) >= c:
                        continue
                    e.wait_ge(sems[sk], c)
                    waited[sk] = c
                if op['fn'] is None:
                    continue
                ins = op['fn'](e)
                if op['sig'] is not None:
                    ins.then_inc(sems[op['sig'][0]], 16 if op['dkey'] is not None else 1)

        @block.tensor
        def _(e):
            run('pe', e)

        @block.scalar
        def _(e):
            run('act', e)

        @block.vector
        def _(e):
            run('dve', e)

        @block.gpsimd
        def _(e):
            run('pool', e)

        @block.sync
        def _(e):
            run('sp', e)


class SbAlloc:
    def __init__(self, nc, limit):
        self.nc = nc
        self.off = 0
        self.limit = limit
        self.n = 0

    def alloc(self, name, shape, dtype):
        sz = int(np.prod(shape[1:])) * (2 if dtype == BF16 else 4)
        sz = (sz + 63) // 64 * 64
        self.n += 1
        t = self.nc.alloc_sbuf_tensor_at("%s_%d" % (name, self.n), list(shape), dtype, offset=16512 + self.off)
        self.off += sz
        assert self.off <= self.limit, (name, self.off)
        return t


def build_program(debug=False, do_delta=True, do_attn=True, stop_after=None):
    nc = bass.Bass("TRN2", target_bir_lowering=False)
    dt = nc.dram_tensor
    xT_d = dt("xT", [D, S_LEN], F32, kind="ExternalInput").ap()
    x_d = dt("x", [S_LEN, D], F32, kind="ExternalInput").ap()
    win_d = dt("w_in", [D, NCOL], F32, kind="ExternalInput").ap()
    wo_d = dt("w_o", [D, D], F32, kind="ExternalInput").ap()
    wup_d = dt("w_up", [D, DFF], F32, kind="ExternalInput").ap()
    wdn_d = dt("w_down", [DFF, D], F32, kind="ExternalInput").ap()
    convw_d = dt("convw", [128, 96], F32, kind="ExternalInput").ap()
    alog_d = dt("alog_r", [128, 128], F32, kind="ExternalInput").ap()
    dtb_d = dt("dtb_r", [128, 128], F32, kind="ExternalInput").ap()
    normw_d = dt("normw", [128, 1], F32, kind="ExternalInput").ap()
    sinks_d = dt("sinks_r", [128, 16], F32, kind="ExternalInput").ap()
    biasT_d = dt("biasT", [128, 16 * 256], F32, kind="ExternalInput").ap()
    lnp_d = dt("lnp", [4, 128, D], F32, kind="ExternalInput").ap()
    cst_d = dt("cst", [128, 8 * 128], F32, kind="ExternalInput").ap()
    out_d = dt("out", [S_LEN, D], F32, kind="ExternalOutput").ap()
    x1a_d = dt("x1a", [S_LEN, D], F32, kind="Internal").ap()
    mixA_d = dt("mixA", [128, 8, S_LEN], BF16, kind="Internal").ap()
    if debug:
        dbg_mix = dt("dbg_mix", [16, 128, S_LEN], BF16, kind="ExternalOutput").ap()
        dbg_s = dt("dbg_s", [8, 128, 128], F32, kind="ExternalOutput").ap()

    S = Sched()
    marks = {}
    with ExitStack() as st:
        SB = SbAlloc(nc, 229376 - 16512)
        pbank = [st.enter_context(nc.psum_tensor("pb%d" % i, [128, 512], F32)) for i in range(2)]
        pS2 = st.enter_context(nc.psum_tensor("pS2", [128, 1024], F32))
        pbank += [pS2[:, 0:512], pS2[:, 512:1024]]
        pbank += [st.enter_context(nc.psum_tensor("pb%d" % i, [128, 512], F32)) for i in range(4, 8)]

        def dma(eng, out, in_, reads, writes, dkey):
            S.add(eng, lambda e: e.dma_start(out=out, in_=in_), reads, writes, dkey)

        def mm(out, lhsT, rhs, start, stop, reads, writes):
            S.add('pe', lambda e: e.matmul(out, lhsT, rhs, start=start, stop=stop), reads, writes)

        def tr(out, in_, ident, reads, writes):
            S.add('pe', lambda e: e.transpose(out, in_, ident), reads, writes)

        def act(out, in_, func, reads, writes, bias=0.0, scale=1.0, accum=None):
            if accum is None:
                S.add('act', lambda e: e.activation(out, in_, func, bias=bias, scale=scale), reads, writes)
            else:
                S.add('act', lambda e: e.activation(out, in_, func, bias=bias, scale=scale, accum_out=accum),
                      reads, writes)

        def ts(eng, out, in0, s1, s2, op0, op1, reads, writes):
            if s2 is None:
                S.add(eng, lambda e: e.tensor_scalar(out, in0, s1, None, op0), reads, writes)
            else:
                S.add(eng, lambda e: e.tensor_scalar(out, in0, s1, s2, op0, op1), reads, writes)

        def tt(eng, out, in0, in1, op, reads, writes):
            S.add(eng, lambda e: e.tensor_tensor(out, in0, in1, op), reads, writes)

        def stt(eng, out, in0, sc, in1, op0, op1, reads, writes):
            S.add(eng, lambda e: e.scalar_tensor_tensor(out, in0, sc, in1, op0, op1), reads, writes)

        def cp(eng, out, in_, reads, writes):
            if eng == 'act':
                S.add('act', lambda e: e.copy(out, in_), reads, writes)
            else:
                S.add(eng, lambda e: e.tensor_copy(out, in_), reads, writes)

        xT = SB.alloc("xT", [128, 16, S_LEN], BF16)
        mixT = SB.alloc("mixT", [128, 16, S_LEN], BF16)
        cstf = SB.alloc("cstf", [128, 5, 128], F32)
        cstb = SB.alloc("cstb", [128, 8, 128], BF16)
        base_mark = SB.off
        ident_f, U_f, ones_f = cstf[:, 0, :], cstf[:, 1, :], cstf[:, 4, :]
        ident_b, Ls_b, Uns_b, ones_b = cstb[:, 0, :], cstb[:, 2, :], cstb[:, 3, :], cstb[:, 4, :]

        dma('sp', cstf[:], cst_d[:, 0:640].rearrange("p (a b) -> p a b", a=5), [], ['cstf'], 'cst')
        dma('pool', cstb[:], cst_d.rearrange("p (a b) -> p a b", a=8), [], ['cstb'], 'cstb')
        BD_b, O1_b, O2_b = cstb[:, 5, :], cstb[:, 6, :], cstb[:, 7, :]
        for kc in range(16):
            dma('pool', xT[:, kc, :], xT_d[kc * 128:(kc + 1) * 128, :], [], [('xT', kc)], ('xT', kc))
        marks['load'] = len(S.ops)

        NWB = 3
        wbuf = [SB.alloc("wb", [128, 16, 128], BF16) for _ in range(NWB)]
        wstate = {'i': 0}

        def load_wtile(col_specs):
            b = wstate['i'] % NWB
            wstate['i'] += 1
            for (d0, s0, n) in col_specs:
                for k4 in range(4):
                    dma('pool', wbuf[b][:, 4 * k4:4 * k4 + 4, d0:d0 + n],
                        win_d[512 * k4:512 * (k4 + 1), s0:s0 + n].rearrange("(k p) n -> p k n", p=128),
                        [], [('wb', b)], ('wb', b))
            return wbuf[b], ('wb', b)

        pj = {'i': 0}

        def proj_F(wt, wkey, ncolM, evac):
            for tc in range(4):
                b = pj['i'] % 2
                pj['i'] += 1
                pk = ('pb', b)
                for kc in range(16):
                    mm(pbank[b][0:ncolM, :], wt[:, kc, 0:ncolM], xT[:, kc, tc * 512:(tc + 1) * 512],
                       kc == 0, kc == 15, [wkey, ('xT', kc)], [pk])
                evac(tc, pbank[b][0:ncolM, :], pk)

        attn_mark = SB.off
        if do_attn:
            Vall = SB.alloc("Vall", [128, 16, 4, 65], BF16)
            QT = SB.alloc("QT", [128, 2, S_LEN], BF16)
            KT = SB.alloc("KT", [128, S_LEN], BF16)
            biasT = SB.alloc("biasT", [128, 4, 2, 128], F32)
            esink = SB.alloc("esink", [128, 16], F32)
            ssb = [SB.alloc("ssb", [128, 2, 2, 128], F32) for _ in range(2)]
            PT = [SB.alloc("PT", [128, 2, 2, 128], BF16) for _ in range(2)]
            ablk = [SB.alloc("ablk", [128, 256], BF16) for _ in range(2)]
            den = [SB.alloc("den", [128, 4], F32) for _ in range(2)]
            wv = SB.alloc("wv", [128, 16, 256], BF16)

            dma('sp', esink[:], sinks_d, [], ['esink'], 'cst')
            act(esink[:], esink[:], AF.Exp, ['esink'], ['esink'])
            S.add('dve', lambda e: e.memset(Vall[:, :, :, 64:65], 1.0), [], ['Vones'])
            for k4 in range(4):
                dma('pool', wv[:, 4 * k4:4 * k4 + 4, :],
                    win_d[512 * k4:512 * (k4 + 1), C_V:C_V + 256].rearrange("(k p) n -> p k n", p=128), [], ['wv'], 'wv')
            for tb in range(16):
                b = pj['i'] % 2
                pj['i'] += 1
                pk = ('pb', b)
                for kc in range(16):
                    mm(pbank[b][:, 0:256], xT[:, kc, tb * 128:(tb + 1) * 128], wv[:, kc, :],
                       kc == 0, kc == 15, ['wv', ('xT', kc)], [pk])
                cp('act', Vall[:, tb, :, 0:64], pbank[b][:, 0:256].rearrange("p (g d) -> p g d", g=4),
                   [pk], [('V', tb)])
            marks['V'] = len(S.ops)
            it = 0
            for g in range(4):
                dma('sp', biasT[:], biasT_d[:, g * 1024:(g + 1) * 1024].rearrange("p (h k q) -> p h k q", h=4, k=2),
                    [], ['biasT'], 'biasT')
                for i in range(2):
                    wt, wk = load_wtile([(0, C_Q + 256 * g + 128 * i, 128)])
                    proj_F(wt, wk, 128, lambda tc, ps, pk, i=i: cp(
                        'act' if tc % 2 else 'dve', QT[:, i, tc * 512:(tc + 1) * 512], ps, [pk], [('QT', i, tc)]))
                wt, wk = load_wtile([(0, C_K + 64 * g, 64), (64, C_K + 64 * g, 64)])
                proj_F(wt, wk, 128, lambda tc, ps, pk: cp(
                    'act' if tc % 2 else 'dve', KT[:, tc * 512:(tc + 1) * 512], ps, [pk], [('KT', tc)]))
                marks.setdefault('proj0', len(S.ops))
                for n in range(16):
                    marks.setdefault('blk%d' % n, len(S.ops))
                    kbs = [1] if n == 0 else [0, 1]
                    ab = ablk[n % 2]
                    abk = ('ablk', n % 2)
                    for i in range(2):
                        r = it % 2
                        it += 1
                        sp_b = ('S', r)
                        op_b = 4 + r
                        S_ps = pS2[:, :].rearrange("p (j r k q) -> p j r k q", j=2, r=2, k=2)[:, :, r, :, :]
                        O_ps = pbank[op_b][:, 0:130].rearrange("p (j d) -> p j d", j=2)
                        for j in range(2):
                            for kb in kbs:
                                kblk = n - 1 + kb
                                mm(S_ps[:, j, kb, :], KT[64 * j:64 * j + 64, kblk * 128:(kblk + 1) * 128],
                                   QT[64 * j:64 * j + 64, i, n * 128:(n + 1) * 128], True, True,
                                   [('KT', kblk // 4), ('QT', i, n // 4)], [('pb', sp_b)])
                        marks.setdefault('b0a', len(S.ops))
                        k0 = kbs[0]
                        stt('dve', ssb[r][:, :, k0:2, :], S_ps[:, :, k0:2, :], 0.125,
                            biasT[:, 2 * i:2 * i + 2, k0:2, :], ALU.mult, ALU.add,
                            [('pb', sp_b), 'biasT'], [('ssb', r)])
                        marks.setdefault('b0b', len(S.ops))
                        act(PT[r][:, :, k0:2, :], ssb[r][:, :, k0:2, :], AF.Exp, [('ssb', r)], [('PT', r)])
                        for j in range(2):
                            for kb in kbs:
                                kblk = n - 1 + kb
                                mm(O_ps[:, j, :], PT[r][:, j, kb, :], Vall[:, kblk, g, :], kb == kbs[0], kb == 1,
                                   [('PT', r), ('V', kblk), 'Vones'], [('pb', op_b)])
                        marks.setdefault('b0d', len(S.ops))
                        h0 = 4 * g + 2 * i
                        tt('dve', den[r][:, 0:2], O_ps[:, :, 64], esink[:, h0:h0 + 2], ALU.add,
                           [('pb', op_b), 'esink'], [('den', r)])
                        S.add('dve', lambda e, r=r: e.reciprocal(den[r][:, 2:4], den[r][:, 0:2]),
                              [('den', r)], [('rden', r)])
                        marks.setdefault('b0f', len(S.ops))
                        for j in range(2):
                            act(ab[:, (2 * i + j) * 64:(2 * i + j + 1) * 64], O_ps[:, j, 0:64], AF.Copy,
                                [('pb', op_b), ('rden', r)], [abk], scale=den[r][:, 2 + j:3 + j])
                    marks.setdefault('b0h', len(S.ops))
                    T_ps = pbank[6][:, :].bitcast(BF16)[:, 0:256].rearrange("p (c t) -> p c t", c=2)
                    for c in range(2):
                        tr(T_ps[:, c, :], ab[:, c * 128:(c + 1) * 128], ident_b, [abk, 'cstb'], [('pb', 6)])
                    cp('dve', mixT[:, 2 * g:2 * g + 2, n * 128:(n + 1) * 128], T_ps, [('pb', 6)], [('mixT', n)])
        else:
            for n in range(16):
                S.add('dve', lambda e, n=n: e.memset(mixT[:, 0:8, n * 128:(n + 1) * 128], 0.0), [], [('mixT', n)])


        def delta_stage():
            def sc(nm):
                return SB.alloc(nm, [128, 16, 8], F32)
            araw, braw, t1, gsc, beta, Gc, eG, decl, glast, bG, alog, dtb = [sc("sc%d" % i) for i in range(12)]
            convw = SB.alloc("convw", [128, 96], F32)
            normw = SB.alloc("normw", [128, 1], F32)
            wab = SB.alloc("wab", [128, 16, 16], BF16)
            qTh, kTh, vTh, szT = [SB.alloc(n_, [128, S_LEN], BF16) for n_ in ("qTh", "kTh", "vTh", "szT")]
            xraw = [SB.alloc("xraw", [128, 1028], F32) for _ in range(2)]
            carry = SB.alloc("carry", [128, 4], F32)
            cacc = SB.alloc("cacc", [128, 1024], F32)
            sq = SB.alloc("sq", [128, 512], BF16)
            rn = SB.alloc("rn", [128, 512], F32)
            B = 4
            shn = ('D', 'DT', 'eGr', 'Af', 'A0', 'A1', 'M0', 'M1', 'P0', 'P1')
            lgn = ('Rv', 'Rw', 'kdec', 'qdT', 'AiT', 'TT', 'nWT')
            fpn = ('gB', 'tmax', 'tmin')
            onen = ('vnew', 'junk', 'onrm')
            LOCALK = set(shn) | set(lgn) | set(fpn) | set(onen) | {'sm0', 'sm1', 'sm2'}
            shs, lgs, fps, ones_t, sms = [], [], [], [], []
            for st_ in range(2):
                if st_ == 0:
                    sv_off = SB.off
                    SB.off = 65536
                shs.append({n_: SB.alloc("sh" + n_, [128, B, 128], BF16) for n_ in shn})
                lgs.append({n_: SB.alloc("lg" + n_, [128, B, 128], BF16) for n_ in lgn})
                fps.append({n_: SB.alloc("fp" + n_, [128, B, 128], F32) for n_ in fpn})
                ones_t.append({n_: SB.alloc(n_, [128, 128], BF16) for n_ in onen})
                sms.append(SB.alloc("small", [128, 4], F32))
                if st_ == 0:
                    assert SB.off <= 65536 + 32768, SB.off
                    SB.off = sv_off
            Sf = SB.alloc("Sf", [128, 128], F32)
            Sbf = SB.alloc("Sbf", [128, 128], BF16)
            cur = {'st': 0}
            S_add_orig = S.add

            def mk(keys):
                return [(k_, cur['st']) if (isinstance(k_, str) and k_ in LOCALK) else k_ for k_ in keys]

            def add_mapped(eng, fn, reads=(), writes=(), dkey=None):
                return S_add_orig(eng, fn, mk(reads), mk(writes), dkey)
            sck = 'scal'

            def f3(ap):
                return ap.rearrange("p (a b) -> p a b", a=16)

            def f2(t_):
                return t_[:].rearrange("p a b -> p (a b)")

            dma('sp', alog[:], f3(alog_d), [], ['alog'], 'dcst')
            dma('sp', dtb[:], f3(dtb_d), [], ['dtb'], 'dcst')
            dma('sp', convw[:], convw_d, [], ['convw'], 'dcst')
            dma('sp', normw[:], normw_d, [], ['normw'], 'dcst')
            for k4 in range(4):
                dma('pool', wab[:, 4 * k4:4 * k4 + 4, :],
                    win_d[512 * k4:512 * (k4 + 1), C_A:C_A + 16].rearrange("(k p) n -> p k n", p=128), [], ['wab'], 'wab')
            P2 = pbank[2]
            for blk in range(16):
                for kc in range(16):
                    mm(P2[:, blk * 16:(blk + 1) * 16], xT[:, kc, blk * 128:(blk + 1) * 128], wab[:, kc, :],
                       kc == 0, kc == 15, ['wab', ('xT', kc)], [('pb', 2)])
            ab3 = P2[:, 0:256].rearrange("p (a b) -> p a b", a=16)
            cp('dve', araw[:], ab3[:, :, 0:8], [('pb', 2)], ['araw'])
            cp('dve', braw[:], ab3[:, :, 8:16], [('pb', 2)], ['braw'])
            tt('dve', araw[:], araw[:], dtb[:], ALU.add, ['araw', 'dtb'], ['araw'])
            stt('dve', t1[:], araw[:], -1.0, araw[:], ALU.mult, ALU.max, ['araw'], ['t1'])
            act(t1[:], t1[:], AF.Exp, ['t1'], ['t1'], scale=-1.0)
            act(t1[:], t1[:], AF.Ln, ['t1'], ['t1'], bias=1.0)
            stt('dve', t1[:], araw[:], 0.0, t1[:], ALU.max, ALU.add, ['araw', 't1'], ['t1'])
            act(alog[:], alog[:], AF.Exp, ['alog'], ['alog'])
            stt('dve', gsc[:], t1[:], -1.0, alog[:], ALU.mult, ALU.mult, ['t1', 'alog'], ['gsc'])
            act(braw[:], braw[:], AF.Exp, ['braw'], ['braw'], scale=-1.0)
            ts('dve', braw[:], braw[:], 1.0, None, ALU.add, None, ['braw'], ['braw'])
            S.add('dve', lambda e: e.reciprocal(beta[:], braw[:]), ['braw'], ['beta'])
            for blk in range(16):
                mm(P2[:, 256 + blk * 8:264 + blk * 8], U_f, gsc[:, blk, :], True, True, ['cstf', 'gsc'], [('pb', 2)])
                mm(P2[:, 384 + blk * 8:392 + blk * 8], ones_f, gsc[:, blk, :], True, True, ['cstf', 'gsc'], [('pb', 2)])
            cp('dve', f2(Gc), P2[:, 256:384], [('pb', 2)], ['Gc'])
            cp('dve', f2(t1), P2[:, 384:512], [('pb', 2), 't1'], ['t1'])
            act(f2(eG), f2(Gc), AF.Exp, ['Gc'], ['eG'])
            act(f2(glast), f2(t1), AF.Exp, ['t1'], ['glast'])
            tt('dve', f2(decl), f2(t1), f2(Gc), ALU.subtract, ['t1', 'Gc'], ['decl'])
            act(f2(decl), f2(decl), AF.Exp, ['decl'], ['decl'])
            tt('dve', f2(bG), f2(beta), f2(eG), ALU.mult, ['beta', 'eG'], ['bG'])
            S.add('dve', None, ['beta', 'bG', 'decl', 'gsc', 'Gc', 'glast'], [])
            SCAL = ['beta', 'bG', 'decl', 'gsc', 'Gc', 'glast']
            marks['d0'] = len(S.ops)

            pS2b = pS2[:, :].bitcast(BF16)
            bank_f = {2: pbank[2], 3: pbank[3]}
            bank_b = {2: pS2b[:, 0:1024], 3: pS2b[:, 1024:2048]}
            for b_ in range(4, 8):
                bank_f[b_] = pbank[b_][:, :]
                bank_b[b_] = pbank[b_][:, :].bitcast(BF16)
            rot = {'i': 0}

            def pbk():
                b_ = 2 + rot['i'] % 6
                rot['i'] += 1
                return (bank_f[b_].rearrange("p (b q) -> p b q", b=4),
                        bank_b[b_].rearrange("p (b q) -> p b q", b=8), ('pb', b_))

            def col(t, c, h):
                return t[:, c, h:h + 1]

            def preprocess(h):
                wt, wk = load_wtile([(0, C_Z + h * 128, 128)])
                proj_F(wt, wk, 128, lambda tc, ps, pk: act(szT[:, tc * 512:(tc + 1) * 512], ps, AF.Silu, [pk],
                                                           [('szT', tc)]))
                units = []
                for c3, dst, dn in ((0, qTh, 'qT'), (1, kTh, 'kT'), (2, vTh, 'vT')):
                    for hf in range(2):
                        units.append((c3, dst, dn, hf))
                wts = {}

                def stage_P(u):
                    c3, dst, dn, hf = units[u]
                    if hf == 0:
                        wts[c3] = load_wtile([(0, C_QKVD + c3 * 1024 + h * 128, 128)])
                    wt_, wk_ = wts[c3]
                    xr = xraw[u % 2]
                    xc, x0, x1 = ('xr_c', u % 2), ('xr', u % 2, 0), ('xr', u % 2, 1)
                    if hf == 0:
                        S.add('dve', lambda e: e.memset(xr[:, 0:3], 0.0), [], [xc])
                    else:
                        cp('dve', xr[:, 0:3], carry[:, 0:3], ['carry'], [xc])
                    for t2 in range(2):
                        tc = hf * 2 + t2
                        b_ = pj['i'] % 2
                        pj['i'] += 1
                        for kc in range(16):
                            mm(pbank[b_][:, :], wt_[:, kc, :], xT[:, kc, tc * 512:(tc + 1) * 512], kc == 0, kc == 15,
                               [wk_, ('xT', kc)], [('pb', b_)])
                        cp('act', xr[:, 3 + t2 * 512:3 + (t2 + 1) * 512], pbank[b_][:, :], [('pb', b_)], [(x0, x1)[t2]])
                    if hf == 0:
                        cp('dve', carry[:, 0:3], xr[:, 1024:1027], [x1], ['carry'])

                def stage_E(u):
                    c3, dst, dn, hf = units[u]
                    ct = c3 * 8 + h
                    xr, ca = xraw[u % 2], cacc
                    xkeys = [('xr_c', u % 2), ('xr', u % 2, 0), ('xr', u % 2, 1)]
                    ts('dve', ca[:, :], xr[:, 3:1027], convw[:, ct * 4 + 3:ct * 4 + 4], None, ALU.mult, None,
                       xkeys + ['convw'], ['cacc'])
                    for j in (2, 1, 0):
                        stt('dve', ca[:, :], xr[:, j:j + 1024], convw[:, ct * 4 + j:ct * 4 + j + 1], ca[:, :],
                            ALU.mult, ALU.add, xkeys + ['convw', 'cacc'], ['cacc'])
                    if c3 == 2:
                        act(vTh[:, hf * 1024:(hf + 1) * 1024], ca[:, :], AF.Silu, ['cacc'],
                            [('vT', hf * 2), ('vT', hf * 2 + 1)])
                    else:
                        act(ca[:, :], ca[:, :], AF.Silu, ['cacc'], ['cacc'])
                        for t2 in range(2):
                            tc = hf * 2 + t2
                            act(sq[:, :], ca[:, t2 * 512:(t2 + 1) * 512], AF.Square, ['cacc'], ['sq'])
                            mm(P2[:, :], ones_b, sq[:, :], True, True, ['sq', 'cstb'], [('pb', 2)])
                            act(rn[:, :], P2[:, :], AF.Ln, [('pb', 2)], ['rn'], bias=RMS_EPS)
                            act(rn[:, :], rn[:, :], AF.Exp, ['rn'], ['rn'], scale=-0.5)
                            stt('dve', dst[:, tc * 512:(tc + 1) * 512], ca[:, t2 * 512:(t2 + 1) * 512],
                                (128 ** -0.5) if c3 == 0 else 1.0, rn[:, :], ALU.mult, ALU.mult, ['cacc', 'rn'],
                                [(dn, tc)])

                stage_P(0)
                for u in range(len(units)):
                    if u + 1 < len(units):
                        stage_P(u + 1)
                    stage_E(u)

            def flat(t_):
                return t_[:].rearrange("p b q -> p (b q)")

            def batch_gen(h, c0, st, tok):
                sh, lg, fpt = shs[st], lgs[st], fps[st]
                kq = c0 // 4
                tsl = slice(c0 * 128, (c0 + B) * 128)

                def cs(b_):
                    return slice((c0 + b_) * 128, (c0 + b_ + 1) * 128)
                _, Tb8, Tk = pbk()
                for b_ in range(B):
                    tr(Tb8[:, 2 * b_, :], kTh[:, cs(b_)], ident_b, [('kT', kq), 'cstb'], [Tk])
                    tr(Tb8[:, 2 * b_ + 1, :], vTh[:, cs(b_)], ident_b, [('vT', kq), 'cstb'], [Tk])
                for b_ in range(B):
                    c = c0 + b_
                    act(lg['Rv'][:, b_, :], Tb8[:, 2 * b_ + 1, :], AF.Copy, [Tk] + SCAL, ['Rv'], scale=col(beta, c, h))
                    act(lg['Rw'][:, b_, :], Tb8[:, 2 * b_, :], AF.Copy, [Tk] + SCAL, ['Rw'], scale=col(bG, c, h))
                    act(lg['kdec'][:, b_, :], Tb8[:, 2 * b_, :], AF.Copy, [Tk] + SCAL, ['kdec'], scale=col(decl, c, h))
                yield
                Gf, _, Gk = pbk()
                for b_ in range(B):
                    ts('dve', fpt['gB'][:, b_, :], ones_f, col(gsc, c0 + b_, h), None, ALU.mult, None,
                       ['cstf'] + SCAL, ['gB'])
                for b_ in range(B):
                    mm(Gf[:, b_, :], fpt['gB'][:, b_, :], U_f, True, True, ['gB', 'cstf'], [Gk])
                for b_ in range(B):
                    c = c0 + b_
                    ts('dve', fpt['tmax'][:, b_, :], Gf[:, b_, :], col(Gc, c, h), 0.0, ALU.subtract, ALU.max,
                       [Gk] + SCAL, ['tmax'])
                    ts('dve', fpt['tmin'][:, b_, :], Gf[:, b_, :], col(Gc, c, h), 0.0, ALU.subtract, ALU.min,
                       [Gk] + SCAL, ['tmin'])
                cp('dve', fpt['gB'][:], Gf, [Gk, 'gB'], ['gB'])
                act(flat(sh['D']), flat(fpt['tmax']), AF.Exp, ['tmax'], ['D'], scale=-1.0)
                act(flat(sh['DT']), flat(fpt['tmin']), AF.Exp, ['tmin'], ['DT'])
                act(flat(sh['eGr']), flat(fpt['gB']), AF.Exp, ['gB'], ['eGr'])
                for b_ in range(B):
                    tt('dve', sh['D'][:, b_, :], sh['D'][:, b_, :], Ls_b, ALU.mult, ['D', 'cstb'], ['D'])
                    tt('pool', sh['DT'][:, b_, :], sh['DT'][:, b_, :], Uns_b, ALU.mult, ['DT', 'cstb'], ['DT'])
                tt('pool', flat(lg['qdT']), qTh[:, tsl], flat(sh['eGr']), ALU.mult, [('qT', kq), 'eGr'], ['qdT'])
                yield
                KKf, _, KKk = pbk()
                for b_ in range(B):
                    mm(KKf[:, b_, :], kTh[:, cs(b_)], kTh[:, cs(b_)], True, True, [('kT', kq)], [KKk])
                for b_ in range(B):
                    stt('dve', sh['Af'][:, b_, :], KKf[:, b_, :], col(beta, c0 + b_, h), sh['D'][:, b_, :],
                        ALU.mult, ALU.mult, [KKk, 'D'] + SCAL, ['Af'])
                QKf, _, QKk = pbk()
                for b_ in range(B):
                    mm(QKf[:, b_, :], kTh[:, cs(b_)], qTh[:, cs(b_)], True, True, [('kT', kq), ('qT', kq)], [QKk])
                tt('dve', lg['AiT'][:], QKf, sh['DT'][:], ALU.mult, [QKk, 'DT'], ['AiT'])
                yield
                for b_ in range(B):
                    tt('dve', sh['A0'][:, b_, :], sh['Af'][:, b_, :], BD_b, ALU.mult, ['Af', 'cstb'], ['A0'])
                    tt('pool', sh['D'][:, b_, :], sh['Af'][:, b_, :], O1_b, ALU.mult, ['Af', 'cstb', 'D'], ['D'])
                    tt('pool', sh['DT'][:, b_, :], sh['Af'][:, b_, :], O2_b, ALU.mult, ['Af', 'cstb', 'DT'], ['DT'])
                _, M8, M0k = pbk()
                for b_ in range(B):
                    tr(M8[:, b_, :], sh['A0'][:, b_, :], ident_b, ['A0', 'cstb'], [M0k])
                cp('act', sh['M0'][:], M8[:, 0:B, :], [M0k], ['M0'])
                for b_ in range(B):
                    tt('dve', sh['P0'][:, b_, :], ident_b, sh['M0'][:, b_, :], ALU.subtract, ['M0', 'cstb'], ['P0'])
                yield
                ai, mi, pi = 0, 0, 0
                NLEV = 4
                for k in range(NLEV):
                    last = k == NLEV - 1
                    A_, M_, P_ = sh['A%d' % ai], sh['M%d' % mi], sh['P%d' % pi]
                    Ak_, Mk_, Pk_ = 'A%d' % ai, 'M%d' % mi, 'P%d' % pi
                    an, mn, pn = 1 - ai, 1 - mi, 1 - pi
                    An_, Mn_, Pn_ = sh['A%d' % an], sh['M%d' % mn], sh['P%d' % pn]
                    if not last:
                        Mf, _, Mfk = pbk()
                        for b_ in range(B):
                            mm(Mf[:, b_, :], A_[:, b_, :], M_[:, b_, :], True, True, [Ak_, Mk_], [Mfk])
                    Af, _, Afk = pbk()
                    for b_ in range(B):
                        mm(Af[:, b_, :], M_[:, b_, :], A_[:, b_, :], True, True, [Ak_, Mk_], [Afk])
                    if not last:
                        cp('act', Mn_[:], Mf, [Mfk], ['M%d' % mn])
                    cp('dve', An_[:], Af, [Afk], ['A%d' % an])
                    yield
                    Pf, _, Pfk = pbk()
                    for b_ in range(B):
                        mm(Pf[:, b_, :], ident_b, P_[:, b_, :], True, False, ['cstb', Pk_], [Pfk])
                        mm(Pf[:, b_, :], An_[:, b_, :], P_[:, b_, :], False, True, ['A%d' % an, Pk_], [Pfk])
                    cp('act' if k % 2 else 'dve', Pn_[:], Pf, [Pfk], ['P%d' % pn])
                    yield
                    ai, pi = an, pn
                    if not last:
                        mi = mn
                for lvl, Ao_n in enumerate(('D', 'DT')):
                    Yn_ = 'P%d' % pi
                    Y_ = sh[Yn_]
                    W4, _, Wk_ = pbk()
                    for b_ in range(B):
                        mm(W4[:, b_, :], sh[Ao_n][:, b_, :], Y_[:, b_, :], True, True, [Ao_n, Yn_], [Wk_])
                    cp('dve', sh['Af'][:], W4, [Wk_], ['Af'])
                    _, X8, Xk_ = pbk()
                    for b_ in range(B):
                        tr(X8[:, b_, :], Y_[:, b_, :], ident_b, [Yn_, 'cstb'], [Xk_])
                    act(sh['eGr'][:], X8[:, 0:B, :], AF.Copy, [Xk_], ['eGr'], scale=-1.0)
                    yield
                    Yf, _, Yk_ = pbk()
                    for b_ in range(B):
                        mm(Yf[:, b_, :], ident_b, Y_[:, b_, :], True, False, ['cstb', Yn_], [Yk_])
                        mm(Yf[:, b_, :], sh['eGr'][:, b_, :], sh['Af'][:, b_, :], False, True, ['eGr', 'Af'], [Yk_])
                    yield
                    if lvl == 1:
                        cp('act', lg['TT'][:], Yf, [Yk_], ['TT'])
                    else:
                        pn = 1 - pi
                        cp('dve', sh['P%d' % pn][:], Yf, [Yk_], ['P%d' % pn])
                        pi = pn
                yield
                Wf, _, Wk = pbk()
                for b_ in range(B):
                    mm(Wf[:, b_, :], lg['Rw'][:, b_, :], lg['TT'][:, b_, :], True, True, ['Rw', 'TT'], [Wk])
                act(flat(lg['nWT']), Wf.rearrange("p b q -> p (b q)"), AF.Copy, [Wk], ['nWT'], scale=-1.0)

                yield
                while not tok['done'].get((h, c0 - B), c0 == 0):
                    yield
                vnew, junk, onrm, sm = ones_t[st]['vnew'], ones_t[st]['junk'], ones_t[st]['onrm'], sms[st]
                for b_ in range(B):
                    c = c0 + b_
                    csl = slice(c * 128, (c + 1) * 128)
                    vf, _, vk = pbk()
                    mm(vf[:, 0, :], lg['TT'][:, b_, :], lg['Rv'][:, b_, :], True, False, ['TT', 'Rv'], [vk])
                    mm(vf[:, 0, :], lg['nWT'][:, b_, :], Sbf[:, :], False, True, ['nWT', 'Sbf'], [vk])
                    cp('dve', vnew[:, :], vf[:, 0, :], [vk], ['vnew'])
                    yield
                    of, _, ok = pbk()
                    mm(of[:, 0, :], lg['qdT'][:, b_, :], Sbf[:, :], True, False, ['qdT', 'Sbf'], [ok])
                    mm(of[:, 0, :], lg['AiT'][:, b_, :], vnew[:, :], False, True, ['AiT', 'vnew'], [ok])
                    Spf, _, Sk = pbk()
                    mm(Spf[:, 0, :], lg['kdec'][:, b_, :], vnew[:, :], True, True, ['kdec', 'vnew'], [Sk])
                    stt('dve', Sf[:, :], Sf[:, :], col(glast, c, h), Spf[:, 0, :], ALU.mult, ALU.add,
                        ['Sf', Sk] + SCAL, ['Sf'])
                    cp('act', Sbf[:, :], Sf[:, :], ['Sf'], ['Sbf'])
                    S.add('dve', lambda e, sm=sm: e.memset(sm[:, 0:1], 0.0), [], ['sm0'])
                    act(junk[:, :], of[:, 0, :], AF.Square, [ok, 'sm0'], ['junk', 'sm0'], accum=sm[:, 0:1])
                    act(sm[:, 1:2], sm[:, 0:1], AF.Ln, ['sm0'], ['sm1'], bias=RMS_EPS, scale=1.0 / 128)
                    act(sm[:, 2:3], sm[:, 1:2], AF.Exp, ['sm1'], ['sm2'], scale=-0.5)
                    act(onrm[:, :], of[:, 0, :], AF.Copy, [ok, 'sm2'], ['onrm'], scale=sm[:, 2:3])
                    yield
                    _, T8, T2k = pbk()
                    tr(T8[:, 0, :], onrm[:, :], ident_b, ['onrm', 'cstb'], [T2k])
                    stt('dve', mixT[:, 8 + h, csl], T8[:, 0, :], normw[:, 0:1], szT[:, csl], ALU.mult, ALU.mult,
                        [T2k, 'normw', ('szT', c // 4)], [('mixT', c)])
                tok['done'][(h, c0)] = True

            S.add = add_mapped
            tok = {'done': {}}
            for h in range(8):
                cur['st'] = 0
                preprocess(h)
                marks.setdefault('d1', len(S.ops))
                S.add('dve', lambda e: e.memset(Sf[:, :], 0.0), [], ['Sf'])
                S.add('dve', lambda e: e.memset(Sbf[:, :], 0.0), [], ['Sbf'])
                pending = [(batch_gen(h, c0, (c0 // B) % 2, tok), (c0 // B) % 2) for c0 in range(0, 16, B)]
                active = []
                while pending or active:
                    while len(active) < 2 and pending:
                        active.append(pending.pop(0))
                    for it_ in list(active):
                        cur['st'] = it_[1]
                        try:
                            next(it_[0])
                        except StopIteration:
                            active.remove(it_)
                cur['st'] = 0
            S.add = S_add_orig

        marks['A_end'] = len(S.ops)
        for q4 in range(4):
            dma('sp', mixA_d[:, :, q4 * 512:(q4 + 1) * 512], mixT[:, 0:8, q4 * 512:(q4 + 1) * 512],
                [('mixT', 4 * q4 + i_) for i_ in range(4)], ['mixA'], 'mixA')
        S.barrier()
        SB.off = attn_mark
        if do_delta:
            delta_stage()
        else:
            for n in range(16):
                S.add('dve', lambda e, n=n: e.memset(mixT[:, 8:16, n * 128:(n + 1) * 128], 0.0), [], [('mixT', n)])

        S.barrier()
        for q4 in range(4):
            dma('sp', mixT[:, 0:8, q4 * 512:(q4 + 1) * 512], mixA_d[:, :, q4 * 512:(q4 + 1) * 512], ['mixA'],
                [('mixT', 4 * q4 + i_) for i_ in range(4)], ('mixAr', q4))
        if debug:
            for kc in range(16):
                dma('sp', dbg_mix[kc], mixT[:, kc, :], [('mixT', n) for n in range(16)], [('dbg', kc)], 'dbg')

        marks['attn'] = len(S.ops)
        S.barrier()
        SB.off = base_mark
        wo = xT
        lng = SB.alloc("lng", [128, D], F32)
        lnb = SB.alloc("lnb", [128, D], F32)
        xres = [SB.alloc("xres", [128, D], F32) for _ in range(2)]
        rbuf = [SB.alloc("rbuf", [128, D], F32) for _ in range(3)]
        xbf = [SB.alloc("xbf", [128, D], BF16) for _ in range(2)]
        stats = [SB.alloc("stats", [128, 4, 6], F32) for _ in range(3)]
        mv = [SB.alloc("mv", [128, 4], F32) for _ in range(3)]
        cmark = SB.off

        for q4 in range(4):
            dma('pool', wo[:, 4 * q4:4 * q4 + 4, :],
                wo_d[q4 * 512:(q4 + 1) * 512, :].rearrange("(k p) n -> p k n", p=128), [], [('wo', q4)], ('wo', q4))
        dma('sp', lng[:], lnp_d[0], [], ['lng'], 'lnp')
        dma('sp', lnb[:], lnp_d[1], [], ['lnb'], 'lnp')

        def layer_norm(r_ap, rkeys, gam, bet, gk, bk, stt_t, mv_t, skey, eng2):
            for c in range(4):
                S.add('dve', lambda e, c=c: e.bn_stats(stt_t[:, c, :], r_ap[:, c * 512:(c + 1) * 512]),
                      rkeys, [(skey, 'st', c)])
            S.add('dve', lambda e: e.bn_aggr(mv_t[:, 0:2], stt_t[:, :, :].rearrange("p a b -> p (a b)")),
                  [(skey, 'st', c) for c in range(4)], [(skey, 'mv')])
            act(mv_t[:, 2:3], mv_t[:, 1:2], AF.Ln, [(skey, 'mv')], [(skey, 'ln')], bias=LN_EPS)
            act(mv_t[:, 3:4], mv_t[:, 2:3], AF.Exp, [(skey, 'ln')], [(skey, 'rs')], scale=-0.5)
            ts('dve', r_ap, r_ap, mv_t[:, 0:1], mv_t[:, 3:4], ALU.subtract, ALU.mult,
               rkeys + [(skey, 'mv'), (skey, 'rs')], rkeys)
            tt(eng2, r_ap, r_ap, gam, ALU.mult, rkeys + [gk], rkeys)
            tt(eng2, r_ap, r_ap, bet, ALU.add, rkeys + [bk], rkeys)

        yb = {'i': 0}

        def c_mm(tb):
            r, rx = tb % 3, tb % 2
            dma('sp', xres[rx][:], x_d[tb * 128:(tb + 1) * 128, :], [], [('xres', rx)], ('xres', rx))
            for fc in range(4):
                b = yb['i'] % 6
                yb['i'] += 1
                for kc in range(16):
                    mm(pbank[b][:, :], mixT[:, kc, tb * 128:(tb + 1) * 128], wo[:, kc, fc * 512:(fc + 1) * 512],
                       kc == 0, kc == 15, [('mixT', tb), ('wo', kc // 4)], [('pb', b)])
                stt('dve', rbuf[r][:, fc * 512:(fc + 1) * 512], xres[rx][:, fc * 512:(fc + 1) * 512], ALPHA,
                    pbank[b][:, :], ALU.mult, ALU.add, [('xres', rx), ('pb', b)], [('rbuf', r, fc)])

        def c_ln(tb):
            r = tb % 3
            rks = [('rbuf', r, fc) for fc in range(4)]
            layer_norm(rbuf[r][:], rks, lng[:], lnb[:], 'lng', 'lnb', stats[r], mv[r], ('lnC', r), 'dve')
            cp('act', xbf[tb % 2][:], rbuf[r][:], rks, [('xbf', tb % 2)])

        def c_fin(tb):
            r, rb = tb % 3, tb % 2
            rks = [('rbuf', r, fc) for fc in range(4)]
            for hlf in range(2):
                T_ps = pbank[6 + hlf][:, :].bitcast(BF16).rearrange("p (c t) -> p c t", c=8)
                for c in range(8):
                    kc = hlf * 8 + c
                    tr(T_ps[:, c, :], xbf[rb][:, kc * 128:(kc + 1) * 128], ident_b, [('xbf', rb), 'cstb'],
                       [('pb', 6 + hlf)])
                cp('act', mixT[:, hlf * 8:hlf * 8 + 8, tb * 128:(tb + 1) * 128], T_ps, [('pb', 6 + hlf)],
                   [('mixT', tb)])
            act(rbuf[r][:], rbuf[r][:], AF.Copy, rks + [('xbf', rb)], rks, scale=ALPHA)
            dma('sp', x1a_d[tb * 128:(tb + 1) * 128, :], rbuf[r][:], rks, [('x1a', tb // 4)], ('x1ao', r))

        for tb in range(17):
            if tb >= 1:
                c_ln(tb - 1)
            if tb < 16:
                c_mm(tb)
            if tb >= 1:
                c_fin(tb - 1)

        marks['C'] = len(S.ops)
        S.barrier()
        SB.off = base_mark
        x1T = mixT
        lng2 = SB.alloc("lng2", [128, D], F32)
        lnb2 = SB.alloc("lnb2", [128, D], F32)
        sv = SB.off
        SB.off = 0
        acc = SB.alloc("acc", [128, 4, D], F32)
        wu = [SB.alloc("wu", [128, 16, 256], BF16) for _ in range(3)]
        assert SB.off <= 65536
        SB.off = sv
        hT = [SB.alloc("hT", [128, 8, 512], BF16) for _ in range(2)]
        rl = [SB.alloc("rl", [128, 512], BF16) for _ in range(2)]
        wd = [SB.alloc("wd", [128, 8, 512], BF16) for _ in range(3)]
        stats2 = [SB.alloc("stats2", [128, 4, 6], F32) for _ in range(2)]
        mv2 = [SB.alloc("mv2", [128, 4], F32) for _ in range(2)]
        dma('sp', lng2[:], lnp_d[2], [], ['lng2'], 'lnp')
        dma('sp', lnb2[:], lnp_d[3], [], ['lnb2'], 'lnp')
        wui = 0
        wdi = 0
        ub = 0
        db = 0
        for tq in range(4):
            for sub in range(4):
                t0 = tq * 512 + sub * 128
                dma('sp', acc[:, sub, :], x1a_d[t0:t0 + 128, :], [('x1a', tq)], [('acc', sub)], ('acc', sub))
            for fg in range(8):
                hb = (tq * 8 + fg) % 2
                for pr in range(4):
                    wb = wui % 3
                    wui += 1
                    c0 = fg * 1024 + pr * 256
                    for k4 in range(4):
                        dma('pool', wu[wb][:, 4 * k4:4 * k4 + 4, :],
                            wup_d[512 * k4:512 * (k4 + 1), c0:c0 + 256].rearrange("(k p) n -> p k n", p=128),
                            [], [('wu', wb)], ('wu', wb))
                    for c in range(2):
                        b = ub % 2
                        ub += 1
                        for kc in range(16):
                            mm(pbank[b][:, :], wu[wb][:, kc, c * 128:(c + 1) * 128],
                               x1T[:, kc, tq * 512:(tq + 1) * 512], kc == 0, kc == 15,
                               [('wu', wb)] + [('mixT', 4 * tq + s) for s in range(4)], [('pb', b)])
                        act(rl[b][:], pbank[b][:, :], AF.Relu, [('pb', b)], [('rl', b)])
                        tt('dve', hT[hb][:, pr * 2 + c, :], rl[b][:], rl[b][:], ALU.mult, [('rl', b)],
                           [('hT', hb, pr * 2 + c)])
                for fo in range(4):
                    wb = wdi % 3
                    wdi += 1
                    for c2 in range(2):
                        r0 = fg * 1024 + c2 * 512
                        dma('pool', wd[wb][:, 4 * c2:4 * c2 + 4, :],
                            wdn_d[r0:r0 + 512, fo * 512:(fo + 1) * 512].rearrange("(c p) n -> p c n", p=128),
                            [], [('wd', wb)], ('wd', wb))
                    for sub in range(4):
                        b = 2 + db % 4
                        db += 1
                        for c in range(8):
                            mm(pbank[b][:, :], hT[hb][:, c, sub * 128:(sub + 1) * 128], wd[wb][:, c, :],
                               c == 0, c == 7, [('hT', hb, c), ('wd', wb)], [('pb', b)])
                        tt('dve', acc[:, sub, fo * 512:(fo + 1) * 512], acc[:, sub, fo * 512:(fo + 1) * 512],
                           pbank[b][:, :], ALU.add, [('pb', b), ('acc', sub)], [('acc', sub)])
            for sub in range(4):
                t0 = tq * 512 + sub * 128
                layer_norm(acc[:, sub, :], [('acc', sub)], lng2[:], lnb2[:], 'lng2', 'lnb2', stats2[sub % 2],
                           mv2[sub % 2], ('lnD', sub % 2), 'dve')
                dma('sp', out_d[t0:t0 + 128, :], acc[:, sub, :], [('acc', sub)], [('out', tq, sub)], ('outd', sub))
        S.add('sp', None, [('out', tq, sub) for tq in range(4) for sub in range(4)], [])
        if stop_after is not None:
            del S.ops[marks[stop_after]:]
            S.dma_since = [i for i, o in enumerate(S.ops) if o['dkey'] is not None]
            S.last_on = {}
            for i, o in enumerate(S.ops):
                if o['dkey'] is None and o['fn'] is not None:
                    S.last_on[o['eng']] = i
            S.barrier()
        S.emit(nc, st)
    return nc


def _t5_bucket(n):
    n = np.maximum(n, 0)
    max_exact = 16
    nf = np.maximum(n, 1).astype(np.float32)
    large = max_exact + (np.log(nf / max_exact) / math.log(128 / max_exact) * (32 - max_exact)).astype(np.int32)
    large = np.minimum(large, 31)
    return np.where(n < max_exact, n, large)


def prep_shared(inp):
    f = np.float32
    sh = {}
    sh["w_in"] = np.ascontiguousarray(inp["w_in"][0], dtype=f)
    sh["w_o"] = np.ascontiguousarray(inp["w_o"][0], dtype=f)
    sh["w_up"] = np.ascontiguousarray(inp["w_up"][0], dtype=f)
    sh["w_down"] = np.ascontiguousarray(inp["w_down"][0], dtype=f)
    cw = np.asarray(inp["conv_w"], dtype=f)[0, :, 0, :]
    sh["convw"] = np.ascontiguousarray(cw.reshape(4, 24, 128).transpose(2, 1, 0).reshape(128, 96))
    sh["alog_r"] = np.ascontiguousarray(np.broadcast_to(np.tile(np.asarray(inp["a_log"], f)[0], 16), (128, 128)))
    sh["dtb_r"] = np.ascontiguousarray(np.broadcast_to(np.tile(np.asarray(inp["dt_bias"], f)[0], 16), (128, 128)))
    sh["normw"] = np.ascontiguousarray(np.asarray(inp["delta_norm_w"], f)[0].reshape(128, 1))
    sh["sinks_r"] = np.ascontiguousarray(np.broadcast_to(np.asarray(inp["attn_sinks"], f)[0], (128, 16)))
    rb = np.asarray(inp["rel_bias"], f)
    k = np.arange(128)[:, None, None]
    kb = np.arange(2)[None, :, None]
    q = np.arange(128)[None, None, :]
    dist = q + 128 - (kb * 128 + k)
    valid = (dist >= 0) & (dist < 128)
    bidx = _t5_bucket(dist)
    tab = rb[bidx]
    tab = np.where(valid[..., None], tab, f(-1e30)).astype(f)
    sh["biasT"] = np.ascontiguousarray(tab.transpose(0, 3, 1, 2).reshape(128, 16 * 256))
    lnp = np.stack([np.asarray(inp[n], f)[0] for n in ("ln1_g", "ln1_b", "ln2_g", "ln2_b")])
    sh["lnp"] = np.ascontiguousarray(np.broadcast_to(lnp[:, None, :], (4, 128, D)))
    ii = np.arange(128)
    ident = np.eye(128, dtype=f)
    U = (ii[:, None] <= ii[None, :]).astype(f)
    Ls = (ii[None, :] < ii[:, None]).astype(f)
    Uns = (ii[None, :] >= ii[:, None]).astype(f)
    ones = np.ones((128, 128), f)
    BD = (ii[:, None] // 32 == ii[None, :] // 32).astype(f)
    O1 = ((ii[:, None] // 64 == ii[None, :] // 64).astype(f) - BD)
    O2 = (ii[:, None] // 64 != ii[None, :] // 64).astype(f)
    sh["cst"] = np.ascontiguousarray(np.concatenate([ident, U, Ls, Uns, ones, BD, O1, O2], axis=1))
    return sh


_CACHE = {}


def kernel(**inputs):
    x = np.asarray(inputs["x"], dtype=np.float32)
    sh = prep_shared(inputs)
    in_maps = []
    for b in range(8):
        m = dict(sh)
        m["x"] = np.ascontiguousarray(x[b])
        m["xT"] = np.ascontiguousarray(x[b].T)
        in_maps.append(m)
    if "nc" not in _CACHE:
        _CACHE["nc"] = build_program()
    res = run_bass_kernel_spmd(_CACHE["nc"], in_maps, core_ids=list(range(8)))
    return np.stack([np.asarray(r["out"], dtype=np.float32) for r in res.results], axis=0)
```

```python
import math
import bisect
import numpy as np
import concourse.bass as bass
import concourse.mybir as mybir
from concourse.bass_utils import run_bass_kernel_spmd
from contextlib import ExitStack

F32 = mybir.dt.float32
BF16 = mybir.dt.bfloat16
ALU = mybir.AluOpType
AF = mybir.ActivationFunctionType

D = 2048
S_LEN = 2048
NCOL = 5648
DFF = 8192
ALPHA = 2.0 ** 0.25
LN_EPS = 1e-5
RMS_EPS = 1e-6
C_Q, C_K, C_V, C_QKVD, C_A, C_B, C_Z = 0, 1024, 1280, 1536, 4608, 4616, 4624


class Sched:
    ENG = ('pe', 'act', 'dve', 'pool', 'sp')

    def __init__(self):
        self.ops = []
        self.lw = {}
        self.rd = {}
        self.dkeys = []
        self.last_on = {}
        self.dma_since = []

    def add(self, eng, fn, reads=(), writes=(), dkey=None):
        i = len(self.ops)
        deps = set()
        for k in reads:
            w = self.lw.get(k)
            if w is not None:
                deps.add(w)
        for k in writes:
            w = self.lw.get(k)
            if w is not None:
                deps.add(w)
            r = self.rd.get(k)
            if r:
                deps.update(r.values())
        rk = ('dma', i) if dkey is not None else eng
        for k in reads:
            self.rd.setdefault(k, {})[rk] = i
        for k in writes:
            self.lw[k] = i
            self.rd[k] = {}
        deps.discard(i)
        if dkey is not None:
            if dkey not in self.dkeys:
                self.dkeys.append(dkey)
            self.dma_since.append(i)
        else:
            self.last_on[eng] = i
        self.ops.append(dict(eng=eng, fn=fn, deps=deps, dkey=dkey, sig=None))
        return i

    def barrier(self):
        deps = set(self.last_on.values()) | set(self.dma_since)
        self.dma_since = []
        for e in self.ENG:
            self.ops.append(dict(eng=e, fn=None, deps=set(deps), dkey=None, sig=None))

    def emit(self, nc, stack):
        ops = self.ops

        def skip(dop, op):
            return (dop['dkey'] is None and op['dkey'] is None and dop['eng'] == 'pe'
                    and op['eng'] == 'pe' and op['fn'] is not None)

        signaled = set()
        for op in ops:
            for d in op['deps']:
                if not skip(ops[d], op):
                    signaled.add(d)
        cnt = {e: 0 for e in self.ENG}
        dcnt = {k: 0 for k in self.dkeys}
        dhist = {k: [] for k in self.dkeys}
        for i, op in enumerate(ops):
            if op['dkey'] is not None:
                dcnt[op['dkey']] += 16
                op['sig'] = (('d', op['dkey']), dcnt[op['dkey']])
                dhist[op['dkey']].append(i)
            elif i in signaled:
                cnt[op['eng']] += 1
                op['sig'] = (('e', op['eng']), cnt[op['eng']])
        sems = {}
        for e in self.ENG:
            sems[('e', e)] = stack.enter_context(nc.semaphore("sem_" + e))
        for n, k in enumerate(self.dkeys):
            sems[('d', k)] = stack.enter_context(nc.semaphore("semd_%d" % n))
        block = stack.enter_context(nc.Block())
        per_eng = {e: [] for e in self.ENG}
        for i, op in enumerate(ops):
            per_eng[op['eng']].append(i)

        def run(eng_name, e):
            waited = {}
            for i in per_eng[eng_name]:
                op = ops[i]
                need = {}
                for d in op['deps']:
                    dop = ops[d]
                    if dop['sig'] is None or skip(dop, op):
                        continue
                    sk, c = dop['sig']
                    if dop['dkey'] is not None:
                        hl = dhist[dop['dkey']]
                        c = 16 * bisect.bisect_left(hl, i)
                    if c > need.get(sk, 0):
                        need[sk] = c
                for sk, c in need.items():
                    if waited.get(sk, 0) >= c:
                        continue
                    e.wait_ge(sems[sk], c)
                    waited[sk] = c
                if op['fn'] is None:
                    continue
                ins = op['fn'](e)
                if op['sig'] is not None:
                    ins.then_inc(sems[op['sig'][0]], 16 if op['dkey'] is not None else 1)

        @block.tensor
        def _(e):
            run('pe', e)

        @block.scalar
        def _(e):
            run('act', e)

        @block.vector
        def _(e):
            run('dve', e)

        @block.gpsimd
        def _(e):
            run('pool', e)

        @block.sync
        def _(e):
            run('sp', e)


class SbAlloc:
    def __init__(self, nc, limit):
        self.nc = nc
        self.off = 0
        self.limit = limit
        self.n = 0

    def alloc(self, name, shape, dtype):
        sz = int(np.prod(shape[1:])) * (2 if dtype == BF16 else 4)
        sz = (sz + 63) // 64 * 64
        self.n += 1
        t = self.nc.alloc_sbuf_tensor_at("%s_%d" % (name, self.n), list(shape), dtype, offset=16512 + self.off)
        self.off += sz
        assert self.off <= self.limit, (name, self.off)
        return t


def build_program(debug=False, do_delta=True, do_attn=True, stop_after=None):
    nc = bass.Bass("TRN2", target_bir_lowering=False)
    dt = nc.dram_tensor
    xT_d = dt("xT", [D, S_LEN], F32, kind="ExternalInput").ap()
    x_d = dt("x", [S_LEN, D], F32, kind="ExternalInput").ap()
    win_d = dt("w_in", [D, NCOL], F32, kind="ExternalInput").ap()
    wo_d = dt("w_o", [D, D], F32, kind="ExternalInput").ap()
    wup_d = dt("w_up", [D, DFF], F32, kind="ExternalInput").ap()
    wdn_d = dt("w_down", [DFF, D], F32, kind="ExternalInput").ap()
    convw_d = dt("convw", [128, 96], F32, kind="ExternalInput").ap()
    alog_d = dt("alog_r", [128, 128], F32, kind="ExternalInput").ap()
    dtb_d = dt("dtb_r", [128, 128], F32, kind="ExternalInput").ap()
    normw_d = dt("normw", [128, 1], F32, kind="ExternalInput").ap()
    sinks_d = dt("sinks_r", [128, 16], F32, kind="ExternalInput").ap()
    biasT_d = dt("biasT", [128, 16 * 256], F32, kind="ExternalInput").ap()
    lnp_d = dt("lnp", [4, 128, D], F32, kind="ExternalInput").ap()
    cst_d = dt("cst", [128, 8 * 128], F32, kind="ExternalInput").ap()
    out_d = dt("out", [S_LEN, D], F32, kind="ExternalOutput").ap()
    x1a_d = dt("x1a", [S_LEN, D], F32, kind="Internal").ap()
    mixA_d = dt("mixA", [128, 8, S_LEN], BF16, kind="Internal").ap()
    wupb_d = dt("wupb", [D, DFF], BF16, kind="Internal").ap()
    wdnb_d = dt("wdnb", [DFF, D], BF16, kind="Internal").ap()
    if debug:
        dbg_mix = dt("dbg_mix", [16, 128, S_LEN], BF16, kind="ExternalOutput").ap()
        dbg_s = dt("dbg_s", [8, 128, 128], F32, kind="ExternalOutput").ap()

    S = Sched()
    marks = {}
    with ExitStack() as st:
        SB = SbAlloc(nc, 229376 - 16512)
        pbank = [st.enter_context(nc.psum_tensor("pb%d" % i, [128, 512], F32)) for i in range(2)]
        pS2 = st.enter_context(nc.psum_tensor("pS2", [128, 1024], F32))
        pbank += [pS2[:, 0:512], pS2[:, 512:1024]]
        pbank += [st.enter_context(nc.psum_tensor("pb%d" % i, [128, 512], F32)) for i in range(4, 8)]

        def dma(eng, out, in_, reads, writes, dkey):
            S.add(eng, lambda e: e.dma_start(out=out, in_=in_), reads, writes, dkey)

        def mm(out, lhsT, rhs, start, stop, reads, writes):
            S.add('pe', lambda e: e.matmul(out, lhsT, rhs, start=start, stop=stop), reads, writes)

        def tr(out, in_, ident, reads, writes):
            S.add('pe', lambda e: e.transpose(out, in_, ident), reads, writes)

        def act(out, in_, func, reads, writes, bias=0.0, scale=1.0, accum=None):
            if accum is None:
                S.add('act', lambda e: e.activation(out, in_, func, bias=bias, scale=scale), reads, writes)
            else:
                S.add('act', lambda e: e.activation(out, in_, func, bias=bias, scale=scale, accum_out=accum),
                      reads, writes)

        def ts(eng, out, in0, s1, s2, op0, op1, reads, writes):
            if s2 is None:
                S.add(eng, lambda e: e.tensor_scalar(out, in0, s1, None, op0), reads, writes)
            else:
                S.add(eng, lambda e: e.tensor_scalar(out, in0, s1, s2, op0, op1), reads, writes)

        def tt(eng, out, in0, in1, op, reads, writes):
            S.add(eng, lambda e: e.tensor_tensor(out, in0, in1, op), reads, writes)

        def stt(eng, out, in0, sc, in1, op0, op1, reads, writes):
            S.add(eng, lambda e: e.scalar_tensor_tensor(out, in0, sc, in1, op0, op1), reads, writes)

        def cp(eng, out, in_, reads, writes):
            if eng == 'act':
                S.add('act', lambda e: e.copy(out, in_), reads, writes)
            else:
                S.add(eng, lambda e: e.tensor_copy(out, in_), reads, writes)

        xT = SB.alloc("xT", [128, 16, S_LEN], BF16)
        mixT = SB.alloc("mixT", [128, 16, S_LEN], BF16)
        cstf = SB.alloc("cstf", [128, 5, 128], F32)
        cstb = SB.alloc("cstb", [128, 8, 128], BF16)
        base_mark = SB.off
        ident_f, U_f, ones_f = cstf[:, 0, :], cstf[:, 1, :], cstf[:, 4, :]
        ident_b, Ls_b, Uns_b, ones_b = cstb[:, 0, :], cstb[:, 2, :], cstb[:, 3, :], cstb[:, 4, :]

        dma('sp', cstf[:], cst_d[:, 0:640].rearrange("p (a b) -> p a b", a=5), [], ['cstf'], 'cst')
        dma('pool', cstb[:], cst_d.rearrange("p (a b) -> p a b", a=8), [], ['cstb'], 'cstb')
        BD_b, O1_b, O2_b = cstb[:, 5, :], cstb[:, 6, :], cstb[:, 7, :]
        for kc in range(16):
            dma('pool', xT[:, kc, :], xT_d[kc * 128:(kc + 1) * 128, :], [], [('xT', kc)], ('xT', kc))
        marks['load'] = len(S.ops)

        NWB = 3
        wbuf = [SB.alloc("wb", [128, 16, 128], BF16) for _ in range(NWB)]
        wstate = {'i': 0}

        def load_wtile(col_specs):
            b = wstate['i'] % NWB
            wstate['i'] += 1
            for (d0, s0, n) in col_specs:
                for k4 in range(4):
                    dma('pool', wbuf[b][:, 4 * k4:4 * k4 + 4, d0:d0 + n],
                        win_d[512 * k4:512 * (k4 + 1), s0:s0 + n].rearrange("(k p) n -> p k n", p=128),
                        [], [('wb', b)], ('wb', b))
            return wbuf[b], ('wb', b)

        cvt_jobs = [('u', r_) for r_ in range(16)] + [('d', r_) for r_ in range(64)]

        def cvt_some(n_):
            for _ in range(n_):
                if not cvt_jobs:
                    return
                kind_, r_ = cvt_jobs.pop(0)
                if kind_ == 'u':
                    dma('pool', wupb_d[r_ * 128:(r_ + 1) * 128, :], wup_d[r_ * 128:(r_ + 1) * 128, :], [], ['wupb'], 'cvt')
                else:
                    dma('pool', wdnb_d[r_ * 128:(r_ + 1) * 128, :], wdn_d[r_ * 128:(r_ + 1) * 128, :], [], ['wdnb'], 'cvt')

        pj = {'i': 0}

        def proj_F(wt, wkey, ncolM, evac):
            for tc in range(4):
                b = pj['i'] % 2
                pj['i'] += 1
                pk = ('pb', b)
                for kc in range(16):
                    mm(pbank[b][0:ncolM, :], wt[:, kc, 0:ncolM], xT[:, kc, tc * 512:(tc + 1) * 512],
                       kc == 0, kc == 15, [wkey, ('xT', kc)], [pk])
                evac(tc, pbank[b][0:ncolM, :], pk)

        attn_mark = SB.off
        if do_attn:
            Vall = SB.alloc("Vall", [128, 16, 4, 65], BF16)
            QT = SB.alloc("QT", [128, 2, S_LEN], BF16)
            KT = SB.alloc("KT", [128, S_LEN], BF16)
            biasT = SB.alloc("biasT", [128, 4, 2, 128], F32)
            esink = SB.alloc("esink", [128, 16], F32)
            ssb = [SB.alloc("ssb", [128, 2, 2, 128], F32) for _ in range(2)]
            PT = [SB.alloc("PT", [128, 2, 2, 128], BF16) for _ in range(2)]
            ablk = [SB.alloc("ablk", [128, 256], BF16) for _ in range(2)]
            den = [SB.alloc("den", [128, 4], F32) for _ in range(2)]
            wv = SB.alloc("wv", [128, 16, 256], BF16)

            dma('sp', esink[:], sinks_d, [], ['esink'], 'cst')
            act(esink[:], esink[:], AF.Exp, ['esink'], ['esink'])
            S.add('dve', lambda e: e.memset(Vall[:, :, :, 64:65], 1.0), [], ['Vones'])
            for k4 in range(4):
                dma('pool', wv[:, 4 * k4:4 * k4 + 4, :],
                    win_d[512 * k4:512 * (k4 + 1), C_V:C_V + 256].rearrange("(k p) n -> p k n", p=128), [], ['wv'], 'wv')
            for tb in range(16):
                b = pj['i'] % 2
                pj['i'] += 1
                pk = ('pb', b)
                for kc in range(16):
                    mm(pbank[b][:, 0:256], xT[:, kc, tb * 128:(tb + 1) * 128], wv[:, kc, :],
                       kc == 0, kc == 15, ['wv', ('xT', kc)], [pk])
                cp('act', Vall[:, tb, :, 0:64], pbank[b][:, 0:256].rearrange("p (g d) -> p g d", g=4),
                   [pk], [('V', tb)])
            marks['V'] = len(S.ops)
            it = 0
            for g in range(4):
                dma('sp', biasT[:], biasT_d[:, g * 1024:(g + 1) * 1024].rearrange("p (h k q) -> p h k q", h=4, k=2),
                    [], ['biasT'], 'biasT')
                for i in range(2):
                    wt, wk = load_wtile([(0, C_Q + 256 * g + 128 * i, 128)])
                    proj_F(wt, wk, 128, lambda tc, ps, pk, i=i: cp(
                        'act' if tc % 2 else 'dve', QT[:, i, tc * 512:(tc + 1) * 512], ps, [pk], [('QT', i, tc)]))
                wt, wk = load_wtile([(0, C_K + 64 * g, 64), (64, C_K + 64 * g, 64)])
                proj_F(wt, wk, 128, lambda tc, ps, pk: cp(
                    'act' if tc % 2 else 'dve', KT[:, tc * 512:(tc + 1) * 512], ps, [pk], [('KT', tc)]))
                marks.setdefault('proj0', len(S.ops))
                for n in range(16):
                    marks.setdefault('blk%d' % n, len(S.ops))
                    kbs = [1] if n == 0 else [0, 1]
                    ab = ablk[n % 2]
                    abk = ('ablk', n % 2)
                    for i in range(2):
                        r = it % 2
                        it += 1
                        sp_b = ('S', r)
                        op_b = 4 + r
                        S_ps = pS2[:, :].rearrange("p (j r k q) -> p j r k q", j=2, r=2, k=2)[:, :, r, :, :]
                        O_ps = pbank[op_b][:, 0:130].rearrange("p (j d) -> p j d", j=2)
                        for j in range(2):
                            for kb in kbs:
                                kblk = n - 1 + kb
                                mm(S_ps[:, j, kb, :], KT[64 * j:64 * j + 64, kblk * 128:(kblk + 1) * 128],
                                   QT[64 * j:64 * j + 64, i, n * 128:(n + 1) * 128], True, True,
                                   [('KT', kblk // 4), ('QT', i, n // 4)], [('pb', sp_b)])
                        marks.setdefault('b0a', len(S.ops))
                        k0 = kbs[0]
                        stt('dve', ssb[r][:, :, k0:2, :], S_ps[:, :, k0:2, :], 0.125,
                            biasT[:, 2 * i:2 * i + 2, k0:2, :], ALU.mult, ALU.add,
                            [('pb', sp_b), 'biasT'], [('ssb', r)])
                        marks.setdefault('b0b', len(S.ops))
                        act(PT[r][:, :, k0:2, :], ssb[r][:, :, k0:2, :], AF.Exp, [('ssb', r)], [('PT', r)])
                        for j in range(2):
                            for kb in kbs:
                                kblk = n - 1 + kb
                                mm(O_ps[:, j, :], PT[r][:, j, kb, :], Vall[:, kblk, g, :], kb == kbs[0], kb == 1,
                                   [('PT', r), ('V', kblk), 'Vones'], [('pb', op_b)])
                        marks.setdefault('b0d', len(S.ops))
                        h0 = 4 * g + 2 * i
                        tt('dve', den[r][:, 0:2], O_ps[:, :, 64], esink[:, h0:h0 + 2], ALU.add,
                           [('pb', op_b), 'esink'], [('den', r)])
                        S.add('dve', lambda e, r=r: e.reciprocal(den[r][:, 2:4], den[r][:, 0:2]),
                              [('den', r)], [('rden', r)])
                        marks.setdefault('b0f', len(S.ops))
                        for j in range(2):
                            act(ab[:, (2 * i + j) * 64:(2 * i + j + 1) * 64], O_ps[:, j, 0:64], AF.Copy,
                                [('pb', op_b), ('rden', r)], [abk], scale=den[r][:, 2 + j:3 + j])
                    marks.setdefault('b0h', len(S.ops))
                    T_ps = pbank[6][:, :].bitcast(BF16)[:, 0:256].rearrange("p (c t) -> p c t", c=2)
                    for c in range(2):
                        tr(T_ps[:, c, :], ab[:, c * 128:(c + 1) * 128], ident_b, [abk, 'cstb'], [('pb', 6)])
                    cp('dve', mixT[:, 2 * g:2 * g + 2, n * 128:(n + 1) * 128], T_ps, [('pb', 6)], [('mixT', n)])
        else:
            for n in range(16):
                S.add('dve', lambda e, n=n: e.memset(mixT[:, 0:8, n * 128:(n + 1) * 128], 0.0), [], [('mixT', n)])


        def delta_stage():
            def sc(nm):
                return SB.alloc(nm, [128, 16, 8], F32)
            araw, braw, t1, gsc, beta, Gc, eG, decl, glast, bG, alog, dtb = [sc("sc%d" % i) for i in range(12)]
            convw = SB.alloc("convw", [128, 96], F32)
            normw = SB.alloc("normw", [128, 1], F32)
            wab = SB.alloc("wab", [128, 16, 16], BF16)
            qTh, kTh, vTh, szT = [SB.alloc(n_, [128, S_LEN], BF16) for n_ in ("qTh", "kTh", "vTh", "szT")]
            xraw = [SB.alloc("xraw", [128, 1028], F32) for _ in range(2)]
            carry = SB.alloc("carry", [128, 4], F32)
            cacc = SB.alloc("cacc", [128, 1024], F32)
            sq = SB.alloc("sq", [128, 512], BF16)
            rn = SB.alloc("rn", [128, 512], F32)
            B = 4
            shn = ('D', 'DT', 'eGr', 'Af', 'A0', 'A1', 'M0', 'M1', 'P0', 'P1')
            lgn = ('Rv', 'Rw', 'kdec', 'qdT', 'AiT', 'TT', 'nWT')
            fpn = ('gB', 'tmax', 'tmin')
            onen = ('vnew', 'junk', 'onrm')
            LOCALK = set(shn) | set(lgn) | set(fpn) | set(onen) | {'sm0', 'sm1', 'sm2'}
            shs, lgs, fps, ones_t, sms = [], [], [], [], []
            for st_ in range(2):
                if st_ == 0:
                    sv_off = SB.off
                    SB.off = 65536
                shs.append({n_: SB.alloc("sh" + n_, [128, B, 128], BF16) for n_ in shn})
                lgs.append({n_: SB.alloc("lg" + n_, [128, B, 128], BF16) for n_ in lgn})
                fps.append({n_: SB.alloc("fp" + n_, [128, B, 128], F32) for n_ in fpn})
                ones_t.append({n_: SB.alloc(n_, [128, 128], BF16) for n_ in onen})
                sms.append(SB.alloc("small", [128, 4], F32))
                if st_ == 0:
                    assert SB.off <= 65536 + 32768, SB.off
                    SB.off = sv_off
            Sf = SB.alloc("Sf", [128, 128], F32)
            Sbf = SB.alloc("Sbf", [128, 128], BF16)
            cur = {'st': 0}
            S_add_orig = S.add

            def mk(keys):
                return [(k_, cur['st']) if (isinstance(k_, str) and k_ in LOCALK) else k_ for k_ in keys]

            def add_mapped(eng, fn, reads=(), writes=(), dkey=None):
                return S_add_orig(eng, fn, mk(reads), mk(writes), dkey)
            sck = 'scal'

            def f3(ap):
                return ap.rearrange("p (a b) -> p a b", a=16)

            def f2(t_):
                return t_[:].rearrange("p a b -> p (a b)")

            dma('sp', alog[:], f3(alog_d), [], ['alog'], 'dcst')
            dma('sp', dtb[:], f3(dtb_d), [], ['dtb'], 'dcst')
            dma('sp', convw[:], convw_d, [], ['convw'], 'dcst')
            dma('sp', normw[:], normw_d, [], ['normw'], 'dcst')
            for k4 in range(4):
                dma('pool', wab[:, 4 * k4:4 * k4 + 4, :],
                    win_d[512 * k4:512 * (k4 + 1), C_A:C_A + 16].rearrange("(k p) n -> p k n", p=128), [], ['wab'], 'wab')
            P2 = pbank[2]
            for blk in range(16):
                for kc in range(16):
                    mm(P2[:, blk * 16:(blk + 1) * 16], xT[:, kc, blk * 128:(blk + 1) * 128], wab[:, kc, :],
                       kc == 0, kc == 15, ['wab', ('xT', kc)], [('pb', 2)])
            ab3 = P2[:, 0:256].rearrange("p (a b) -> p a b", a=16)
            cp('dve', araw[:], ab3[:, :, 0:8], [('pb', 2)], ['araw'])
            cp('dve', braw[:], ab3[:, :, 8:16], [('pb', 2)], ['braw'])
            tt('dve', araw[:], araw[:], dtb[:], ALU.add, ['araw', 'dtb'], ['araw'])
            stt('dve', t1[:], araw[:], -1.0, araw[:], ALU.mult, ALU.max, ['araw'], ['t1'])
            act(t1[:], t1[:], AF.Exp, ['t1'], ['t1'], scale=-1.0)
            act(t1[:], t1[:], AF.Ln, ['t1'], ['t1'], bias=1.0)
            stt('dve', t1[:], araw[:], 0.0, t1[:], ALU.max, ALU.add, ['araw', 't1'], ['t1'])
            act(alog[:], alog[:], AF.Exp, ['alog'], ['alog'])
            stt('dve', gsc[:], t1[:], -1.0, alog[:], ALU.mult, ALU.mult, ['t1', 'alog'], ['gsc'])
            act(braw[:], braw[:], AF.Exp, ['braw'], ['braw'], scale=-1.0)
            ts('dve', braw[:], braw[:], 1.0, None, ALU.add, None, ['braw'], ['braw'])
            S.add('dve', lambda e: e.reciprocal(beta[:], braw[:]), ['braw'], ['beta'])
            for blk in range(16):
                mm(P2[:, 256 + blk * 8:264 + blk * 8], U_f, gsc[:, blk, :], True, True, ['cstf', 'gsc'], [('pb', 2)])
                mm(P2[:, 384 + blk * 8:392 + blk * 8], ones_f, gsc[:, blk, :], True, True, ['cstf', 'gsc'], [('pb', 2)])
            cp('dve', f2(Gc), P2[:, 256:384], [('pb', 2)], ['Gc'])
            cp('dve', f2(t1), P2[:, 384:512], [('pb', 2), 't1'], ['t1'])
            act(f2(eG), f2(Gc), AF.Exp, ['Gc'], ['eG'])
            act(f2(glast), f2(t1), AF.Exp, ['t1'], ['glast'])
            tt('dve', f2(decl), f2(t1), f2(Gc), ALU.subtract, ['t1', 'Gc'], ['decl'])
            act(f2(decl), f2(decl), AF.Exp, ['decl'], ['decl'])
            tt('dve', f2(bG), f2(beta), f2(eG), ALU.mult, ['beta', 'eG'], ['bG'])
            S.add('dve', None, ['beta', 'bG', 'decl', 'gsc', 'Gc', 'glast'], [])
            SCAL = ['beta', 'bG', 'decl', 'gsc', 'Gc', 'glast']
            marks['d0'] = len(S.ops)

            pS2b = pS2[:, :].bitcast(BF16)
            bank_f = {2: pbank[2], 3: pbank[3]}
            bank_b = {2: pS2b[:, 0:1024], 3: pS2b[:, 1024:2048]}
            for b_ in range(4, 8):
                bank_f[b_] = pbank[b_][:, :]
                bank_b[b_] = pbank[b_][:, :].bitcast(BF16)
            rot = {'i': 0}

            def pbk():
                b_ = 2 + rot['i'] % 6
                rot['i'] += 1
                return (bank_f[b_].rearrange("p (b q) -> p b q", b=4),
                        bank_b[b_].rearrange("p (b q) -> p b q", b=8), ('pb', b_))

            def col(t, c, h):
                return t[:, c, h:h + 1]

            def preprocess(h):
                wt, wk = load_wtile([(0, C_Z + h * 128, 128)])
                proj_F(wt, wk, 128, lambda tc, ps, pk: act(szT[:, tc * 512:(tc + 1) * 512], ps, AF.Silu, [pk],
                                                           [('szT', tc)]))
                units = []
                for c3, dst, dn in ((0, qTh, 'qT'), (1, kTh, 'kT'), (2, vTh, 'vT')):
                    for hf in range(2):
                        units.append((c3, dst, dn, hf))
                wts = {}

                def stage_P(u):
                    c3, dst, dn, hf = units[u]
                    if hf == 0:
                        wts[c3] = load_wtile([(0, C_QKVD + c3 * 1024 + h * 128, 128)])
                    wt_, wk_ = wts[c3]
                    xr = xraw[u % 2]
                    xc, x0, x1 = ('xr_c', u % 2), ('xr', u % 2, 0), ('xr', u % 2, 1)
                    if hf == 0:
                        S.add('dve', lambda e: e.memset(xr[:, 0:3], 0.0), [], [xc])
                    else:
                        cp('dve', xr[:, 0:3], carry[:, 0:3], ['carry'], [xc])
                    for t2 in range(2):
                        tc = hf * 2 + t2
                        b_ = pj['i'] % 2
                        pj['i'] += 1
                        for kc in range(16):
                            mm(pbank[b_][:, :], wt_[:, kc, :], xT[:, kc, tc * 512:(tc + 1) * 512], kc == 0, kc == 15,
                               [wk_, ('xT', kc)], [('pb', b_)])
                        cp('act', xr[:, 3 + t2 * 512:3 + (t2 + 1) * 512], pbank[b_][:, :], [('pb', b_)], [(x0, x1)[t2]])
                    if hf == 0:
                        cp('dve', carry[:, 0:3], xr[:, 1024:1027], [x1], ['carry'])

                def stage_E(u):
                    c3, dst, dn, hf = units[u]
                    ct = c3 * 8 + h
                    xr, ca = xraw[u % 2], cacc
                    xkeys = [('xr_c', u % 2), ('xr', u % 2, 0), ('xr', u % 2, 1)]
                    ts('dve', ca[:, :], xr[:, 3:1027], convw[:, ct * 4 + 3:ct * 4 + 4], None, ALU.mult, None,
                       xkeys + ['convw'], ['cacc'])
                    for j in (2, 1, 0):
                        stt('dve', ca[:, :], xr[:, j:j + 1024], convw[:, ct * 4 + j:ct * 4 + j + 1], ca[:, :],
                            ALU.mult, ALU.add, xkeys + ['convw', 'cacc'], ['cacc'])
                    if c3 == 2:
                        act(vTh[:, hf * 1024:(hf + 1) * 1024], ca[:, :], AF.Silu, ['cacc'],
                            [('vT', hf * 2), ('vT', hf * 2 + 1)])
                    else:
                        act(ca[:, :], ca[:, :], AF.Silu, ['cacc'], ['cacc'])
                        for t2 in range(2):
                            tc = hf * 2 + t2
                            act(sq[:, :], ca[:, t2 * 512:(t2 + 1) * 512], AF.Square, ['cacc'], ['sq'])
                            mm(P2[:, :], ones_b, sq[:, :], True, True, ['sq', 'cstb'], [('pb', 2)])
                            act(rn[:, :], P2[:, :], AF.Ln, [('pb', 2)], ['rn'], bias=RMS_EPS)
                            act(rn[:, :], rn[:, :], AF.Exp, ['rn'], ['rn'], scale=-0.5)
                            stt('dve', dst[:, tc * 512:(tc + 1) * 512], ca[:, t2 * 512:(t2 + 1) * 512],
                                (128 ** -0.5) if c3 == 0 else 1.0, rn[:, :], ALU.mult, ALU.mult, ['cacc', 'rn'],
                                [(dn, tc)])

                stage_P(0)
                for u in range(len(units)):
                    if u + 1 < len(units):
                        stage_P(u + 1)
                    stage_E(u)

            def flat(t_):
                return t_[:].rearrange("p b q -> p (b q)")

            def batch_gen(h, c0, st, tok):
                sh, lg, fpt = shs[st], lgs[st], fps[st]
                kq = c0 // 4
                tsl = slice(c0 * 128, (c0 + B) * 128)

                def cs(b_):
                    return slice((c0 + b_) * 128, (c0 + b_ + 1) * 128)
                _, Tb8, Tk = pbk()
                for b_ in range(B):
                    tr(Tb8[:, 2 * b_, :], kTh[:, cs(b_)], ident_b, [('kT', kq), 'cstb'], [Tk])
                    tr(Tb8[:, 2 * b_ + 1, :], vTh[:, cs(b_)], ident_b, [('vT', kq), 'cstb'], [Tk])
                for b_ in range(B):
                    c = c0 + b_
                    act(lg['Rv'][:, b_, :], Tb8[:, 2 * b_ + 1, :], AF.Copy, [Tk] + SCAL, ['Rv'], scale=col(beta, c, h))
                    act(lg['Rw'][:, b_, :], Tb8[:, 2 * b_, :], AF.Copy, [Tk] + SCAL, ['Rw'], scale=col(bG, c, h))
                    act(lg['kdec'][:, b_, :], Tb8[:, 2 * b_, :], AF.Copy, [Tk] + SCAL, ['kdec'], scale=col(decl, c, h))
                yield
                Gf, _, Gk = pbk()
                for b_ in range(B):
                    ts('dve', fpt['gB'][:, b_, :], ones_f, col(gsc, c0 + b_, h), None, ALU.mult, None,
                       ['cstf'] + SCAL, ['gB'])
                for b_ in range(B):
                    mm(Gf[:, b_, :], fpt['gB'][:, b_, :], U_f, True, True, ['gB', 'cstf'], [Gk])
                for b_ in range(B):
                    c = c0 + b_
                    ts('dve', fpt['tmax'][:, b_, :], Gf[:, b_, :], col(Gc, c, h), 0.0, ALU.subtract, ALU.max,
                       [Gk] + SCAL, ['tmax'])
                    ts('dve', fpt['tmin'][:, b_, :], Gf[:, b_, :], col(Gc, c, h), 0.0, ALU.subtract, ALU.min,
                       [Gk] + SCAL, ['tmin'])
                cp('dve', fpt['gB'][:], Gf, [Gk, 'gB'], ['gB'])
                act(flat(sh['D']), flat(fpt['tmax']), AF.Exp, ['tmax'], ['D'], scale=-1.0)
                act(flat(sh['DT']), flat(fpt['tmin']), AF.Exp, ['tmin'], ['DT'])
                act(flat(sh['eGr']), flat(fpt['gB']), AF.Exp, ['gB'], ['eGr'])
                for b_ in range(B):
                    tt('dve', sh['D'][:, b_, :], sh['D'][:, b_, :], Ls_b, ALU.mult, ['D', 'cstb'], ['D'])
                    tt('pool', sh['DT'][:, b_, :], sh['DT'][:, b_, :], Uns_b, ALU.mult, ['DT', 'cstb'], ['DT'])
                tt('pool', flat(lg['qdT']), qTh[:, tsl], flat(sh['eGr']), ALU.mult, [('qT', kq), 'eGr'], ['qdT'])
                yield
                KKf, _, KKk = pbk()
                for b_ in range(B):
                    mm(KKf[:, b_, :], kTh[:, cs(b_)], kTh[:, cs(b_)], True, True, [('kT', kq)], [KKk])
                for b_ in range(B):
                    stt('dve', sh['Af'][:, b_, :], KKf[:, b_, :], col(beta, c0 + b_, h), sh['D'][:, b_, :],
                        ALU.mult, ALU.mult, [KKk, 'D'] + SCAL, ['Af'])
                QKf, _, QKk = pbk()
                for b_ in range(B):
                    mm(QKf[:, b_, :], kTh[:, cs(b_)], qTh[:, cs(b_)], True, True, [('kT', kq), ('qT', kq)], [QKk])
                tt('dve', lg['AiT'][:], QKf, sh['DT'][:], ALU.mult, [QKk, 'DT'], ['AiT'])
                yield
                for b_ in range(B):
                    tt('dve', sh['A0'][:, b_, :], sh['Af'][:, b_, :], BD_b, ALU.mult, ['Af', 'cstb'], ['A0'])
                    tt('pool', sh['D'][:, b_, :], sh['Af'][:, b_, :], O1_b, ALU.mult, ['Af', 'cstb', 'D'], ['D'])
                    tt('pool', sh['DT'][:, b_, :], sh['Af'][:, b_, :], O2_b, ALU.mult, ['Af', 'cstb', 'DT'], ['DT'])
                _, M8, M0k = pbk()
                for b_ in range(B):
                    tr(M8[:, b_, :], sh['A0'][:, b_, :], ident_b, ['A0', 'cstb'], [M0k])
                cp('act', sh['M0'][:], M8[:, 0:B, :], [M0k], ['M0'])
                for b_ in range(B):
                    tt('dve', sh['P0'][:, b_, :], ident_b, sh['M0'][:, b_, :], ALU.subtract, ['M0', 'cstb'], ['P0'])
                yield
                ai, mi, pi = 0, 0, 0
                NLEV = 4
                for k in range(NLEV):
                    last = k == NLEV - 1
                    A_, M_, P_ = sh['A%d' % ai], sh['M%d' % mi], sh['P%d' % pi]
                    Ak_, Mk_, Pk_ = 'A%d' % ai, 'M%d' % mi, 'P%d' % pi
                    an, mn, pn = 1 - ai, 1 - mi, 1 - pi
                    An_, Mn_, Pn_ = sh['A%d' % an], sh['M%d' % mn], sh['P%d' % pn]
                    if not last:
                        Mf, _, Mfk = pbk()
                        for b_ in range(B):
                            mm(Mf[:, b_, :], A_[:, b_, :], M_[:, b_, :], True, True, [Ak_, Mk_], [Mfk])
                    Af, _, Afk = pbk()
                    for b_ in range(B):
                        mm(Af[:, b_, :], M_[:, b_, :], A_[:, b_, :], True, True, [Ak_, Mk_], [Afk])
                    if not last:
                        cp('act', Mn_[:], Mf, [Mfk], ['M%d' % mn])
                    cp('dve', An_[:], Af, [Afk], ['A%d' % an])
                    yield
                    Pf, _, Pfk = pbk()
                    for b_ in range(B):
                        mm(Pf[:, b_, :], ident_b, P_[:, b_, :], True, False, ['cstb', Pk_], [Pfk])
                        mm(Pf[:, b_, :], An_[:, b_, :], P_[:, b_, :], False, True, ['A%d' % an, Pk_], [Pfk])
                    cp('act' if k % 2 else 'dve', Pn_[:], Pf, [Pfk], ['P%d' % pn])
                    yield
                    ai, pi = an, pn
                    if not last:
                        mi = mn
                for lvl, Ao_n in enumerate(('D', 'DT')):
                    Yn_ = 'P%d' % pi
                    Y_ = sh[Yn_]
                    W4, _, Wk_ = pbk()
                    for b_ in range(B):
                        mm(W4[:, b_, :], sh[Ao_n][:, b_, :], Y_[:, b_, :], True, True, [Ao_n, Yn_], [Wk_])
                    cp('dve', sh['Af'][:], W4, [Wk_], ['Af'])
                    _, X8, Xk_ = pbk()
                    for b_ in range(B):
                        tr(X8[:, b_, :], Y_[:, b_, :], ident_b, [Yn_, 'cstb'], [Xk_])
                    act(sh['eGr'][:], X8[:, 0:B, :], AF.Copy, [Xk_], ['eGr'], scale=-1.0)
                    yield
                    Yf, _, Yk_ = pbk()
                    for b_ in range(B):
                        mm(Yf[:, b_, :], ident_b, Y_[:, b_, :], True, False, ['cstb', Yn_], [Yk_])
                        mm(Yf[:, b_, :], sh['eGr'][:, b_, :], sh['Af'][:, b_, :], False, True, ['eGr', 'Af'], [Yk_])
                    yield
                    if lvl == 1:
                        cp('act', lg['TT'][:], Yf, [Yk_], ['TT'])
                    else:
                        pn = 1 - pi
                        cp('dve', sh['P%d' % pn][:], Yf, [Yk_], ['P%d' % pn])
                        pi = pn
                yield
                Wf, _, Wk = pbk()
                for b_ in range(B):
                    mm(Wf[:, b_, :], lg['Rw'][:, b_, :], lg['TT'][:, b_, :], True, True, ['Rw', 'TT'], [Wk])
                act(flat(lg['nWT']), Wf.rearrange("p b q -> p (b q)"), AF.Copy, [Wk], ['nWT'], scale=-1.0)

                yield
                while not tok['done'].get((h, c0 - B), c0 == 0):
                    yield
                vnew, junk, onrm, sm = ones_t[st]['vnew'], ones_t[st]['junk'], ones_t[st]['onrm'], sms[st]
                for b_ in range(B):
                    c = c0 + b_
                    csl = slice(c * 128, (c + 1) * 128)
                    vf, _, vk = pbk()
                    mm(vf[:, 0, :], lg['TT'][:, b_, :], lg['Rv'][:, b_, :], True, False, ['TT', 'Rv'], [vk])
                    mm(vf[:, 0, :], lg['nWT'][:, b_, :], Sbf[:, :], False, True, ['nWT', 'Sbf'], [vk])
                    cp('dve', vnew[:, :], vf[:, 0, :], [vk], ['vnew'])
                    yield
                    of, _, ok = pbk()
                    mm(of[:, 0, :], lg['qdT'][:, b_, :], Sbf[:, :], True, False, ['qdT', 'Sbf'], [ok])
                    mm(of[:, 0, :], lg['AiT'][:, b_, :], vnew[:, :], False, True, ['AiT', 'vnew'], [ok])
                    Spf, _, Sk = pbk()
                    mm(Spf[:, 0, :], lg['kdec'][:, b_, :], vnew[:, :], True, True, ['kdec', 'vnew'], [Sk])
                    stt('dve', Sf[:, :], Sf[:, :], col(glast, c, h), Spf[:, 0, :], ALU.mult, ALU.add,
                        ['Sf', Sk] + SCAL, ['Sf'])
                    cp('act', Sbf[:, :], Sf[:, :], ['Sf'], ['Sbf'])
                    S.add('dve', lambda e, sm=sm: e.memset(sm[:, 0:1], 0.0), [], ['sm0'])
                    act(junk[:, :], of[:, 0, :], AF.Square, [ok, 'sm0'], ['junk', 'sm0'], accum=sm[:, 0:1])
                    act(sm[:, 1:2], sm[:, 0:1], AF.Ln, ['sm0'], ['sm1'], bias=RMS_EPS, scale=1.0 / 128)
                    act(sm[:, 2:3], sm[:, 1:2], AF.Exp, ['sm1'], ['sm2'], scale=-0.5)
                    act(onrm[:, :], of[:, 0, :], AF.Copy, [ok, 'sm2'], ['onrm'], scale=sm[:, 2:3])
                    yield
                    _, T8, T2k = pbk()
                    tr(T8[:, 0, :], onrm[:, :], ident_b, ['onrm', 'cstb'], [T2k])
                    stt('dve', mixT[:, 8 + h, csl], T8[:, 0, :], normw[:, 0:1], szT[:, csl], ALU.mult, ALU.mult,
                        [T2k, 'normw', ('szT', c // 4)], [('mixT', c)])
                tok['done'][(h, c0)] = True

            S.add = add_mapped
            tok = {'done': {}}
            for h in range(8):
                cur['st'] = 0
                preprocess(h)
                marks.setdefault('d1', len(S.ops))
                S.add('dve', lambda e: e.memset(Sf[:, :], 0.0), [], ['Sf'])
                S.add('dve', lambda e: e.memset(Sbf[:, :], 0.0), [], ['Sbf'])
                cvt_some(10)
                pending = [(batch_gen(h, c0, (c0 // B) % 2, tok), (c0 // B) % 2) for c0 in range(0, 16, B)]
                active = []
                while pending or active:
                    while len(active) < 2 and pending:
                        active.append(pending.pop(0))
                    for it_ in list(active):
                        cur['st'] = it_[1]
                        try:
                            next(it_[0])
                        except StopIteration:
                            active.remove(it_)
                cur['st'] = 0
            S.add = S_add_orig
            cvt_some(1000)

        marks['A_end'] = len(S.ops)
        for q4 in range(4):
            dma('sp', mixA_d[:, :, q4 * 512:(q4 + 1) * 512], mixT[:, 0:8, q4 * 512:(q4 + 1) * 512],
                [('mixT', 4 * q4 + i_) for i_ in range(4)], ['mixA'], 'mixA')
        S.barrier()
        SB.off = attn_mark
        if do_delta:
            delta_stage()
        else:
            for n in range(16):
                S.add('dve', lambda e, n=n: e.memset(mixT[:, 8:16, n * 128:(n + 1) * 128], 0.0), [], [('mixT', n)])

        S.barrier()
        for q4 in range(4):
            dma('sp', mixT[:, 0:8, q4 * 512:(q4 + 1) * 512], mixA_d[:, :, q4 * 512:(q4 + 1) * 512], ['mixA'],
                [('mixT', 4 * q4 + i_) for i_ in range(4)], ('mixAr', q4))
        if debug:
            for kc in range(16):
                dma('sp', dbg_mix[kc], mixT[:, kc, :], [('mixT', n) for n in range(16)], [('dbg', kc)], 'dbg')

        marks['attn'] = len(S.ops)
        S.barrier()
        SB.off = base_mark
        wo = xT
        lng = SB.alloc("lng", [128, D], F32)
        lnb = SB.alloc("lnb", [128, D], F32)
        xres = [SB.alloc("xres", [128, D], F32) for _ in range(2)]
        rbuf = [SB.alloc("rbuf", [128, D], F32) for _ in range(3)]
        xbf = [SB.alloc("xbf", [128, D], BF16) for _ in range(2)]
        stats = [SB.alloc("stats", [128, 4, 6], F32) for _ in range(3)]
        mv = [SB.alloc("mv", [128, 4], F32) for _ in range(3)]
        cmark = SB.off

        for q4 in range(4):
            dma('pool', wo[:, 4 * q4:4 * q4 + 4, :],
                wo_d[q4 * 512:(q4 + 1) * 512, :].rearrange("(k p) n -> p k n", p=128), [], [('wo', q4)], ('wo', q4))
        dma('sp', lng[:], lnp_d[0], [], ['lng'], 'lnp')
        dma('sp', lnb[:], lnp_d[1], [], ['lnb'], 'lnp')

        def layer_norm(r_ap, rkeys, gam, bet, gk, bk, stt_t, mv_t, skey, eng2):
            for c in range(4):
                S.add('dve', lambda e, c=c: e.bn_stats(stt_t[:, c, :], r_ap[:, c * 512:(c + 1) * 512]),
                      rkeys, [(skey, 'st', c)])
            S.add('dve', lambda e: e.bn_aggr(mv_t[:, 0:2], stt_t[:, :, :].rearrange("p a b -> p (a b)")),
                  [(skey, 'st', c) for c in range(4)], [(skey, 'mv')])
            act(mv_t[:, 2:3], mv_t[:, 1:2], AF.Ln, [(skey, 'mv')], [(skey, 'ln')], bias=LN_EPS)
            act(mv_t[:, 3:4], mv_t[:, 2:3], AF.Exp, [(skey, 'ln')], [(skey, 'rs')], scale=-0.5)
            ts('dve', r_ap, r_ap, mv_t[:, 0:1], mv_t[:, 3:4], ALU.subtract, ALU.mult,
               rkeys + [(skey, 'mv'), (skey, 'rs')], rkeys)
            tt(eng2, r_ap, r_ap, gam, ALU.mult, rkeys + [gk], rkeys)
            tt(eng2, r_ap, r_ap, bet, ALU.add, rkeys + [bk], rkeys)

        yb = {'i': 0}

        def c_mm(tb):
            r, rx = tb % 3, tb % 2
            dma('sp', xres[rx][:], x_d[tb * 128:(tb + 1) * 128, :], [], [('xres', rx)], ('xres', rx))
            for fc in range(4):
                b = yb['i'] % 6
                yb['i'] += 1
                for kc in range(16):
                    mm(pbank[b][:, :], mixT[:, kc, tb * 128:(tb + 1) * 128], wo[:, kc, fc * 512:(fc + 1) * 512],
                       kc == 0, kc == 15, [('mixT', tb), ('wo', kc // 4)], [('pb', b)])
                stt('dve', rbuf[r][:, fc * 512:(fc + 1) * 512], xres[rx][:, fc * 512:(fc + 1) * 512], ALPHA,
                    pbank[b][:, :], ALU.mult, ALU.add, [('xres', rx), ('pb', b)], [('rbuf', r, fc)])

        def c_ln(tb):
            r = tb % 3
            rks = [('rbuf', r, fc) for fc in range(4)]
            layer_norm(rbuf[r][:], rks, lng[:], lnb[:], 'lng', 'lnb', stats[r], mv[r], ('lnC', r), 'dve')
            cp('act', xbf[tb % 2][:], rbuf[r][:], rks, [('xbf', tb % 2)])

        def c_fin(tb):
            r, rb = tb % 3, tb % 2
            rks = [('rbuf', r, fc) for fc in range(4)]
            for hlf in range(2):
                T_ps = pbank[6 + hlf][:, :].bitcast(BF16).rearrange("p (c t) -> p c t", c=8)
                for c in range(8):
                    kc = hlf * 8 + c
                    tr(T_ps[:, c, :], xbf[rb][:, kc * 128:(kc + 1) * 128], ident_b, [('xbf', rb), 'cstb'],
                       [('pb', 6 + hlf)])
                cp('act', mixT[:, hlf * 8:hlf * 8 + 8, tb * 128:(tb + 1) * 128], T_ps, [('pb', 6 + hlf)],
                   [('mixT', tb)])
            act(rbuf[r][:], rbuf[r][:], AF.Copy, rks + [('xbf', rb)], rks, scale=ALPHA)
            dma('sp', x1a_d[tb * 128:(tb + 1) * 128, :], rbuf[r][:], rks, [('x1a', tb // 4)], ('x1ao', r))

        for tb in range(17):
            if tb >= 1:
                c_ln(tb - 1)
            if tb < 16:
                c_mm(tb)
            if tb >= 1:
                c_fin(tb - 1)

        marks['C'] = len(S.ops)
        S.barrier()
        SB.off = base_mark
        x1T = mixT
        lng2 = SB.alloc("lng2", [128, D], F32)
        lnb2 = SB.alloc("lnb2", [128, D], F32)
        sv = SB.off
        SB.off = 0
        acc = SB.alloc("acc", [128, 4, D], F32)
        wu = [SB.alloc("wu", [128, 16, 256], BF16) for _ in range(3)]
        assert SB.off <= 65536
        SB.off = sv
        hT = [SB.alloc("hT", [128, 8, 512], BF16) for _ in range(2)]
        rl = [SB.alloc("rl", [128, 512], BF16) for _ in range(2)]
        wd = [SB.alloc("wd", [128, 8, 512], BF16) for _ in range(3)]
        stats2 = [SB.alloc("stats2", [128, 4, 6], F32) for _ in range(2)]
        mv2 = [SB.alloc("mv2", [128, 4], F32) for _ in range(2)]
        dma('sp', lng2[:], lnp_d[2], [], ['lng2'], 'lnp')
        dma('sp', lnb2[:], lnp_d[3], [], ['lnb2'], 'lnp')
        wui = 0
        wdi = 0
        ub = 0
        db = 0
        for tq in range(4):
            for sub in range(4):
                t0 = tq * 512 + sub * 128
                dma('sp', acc[:, sub, :], x1a_d[t0:t0 + 128, :], [('x1a', tq)], [('acc', sub)], ('acc', sub))
            for fg in range(8):
                hb = (tq * 8 + fg) % 2
                for pr in range(4):
                    wb = wui % 3
                    wui += 1
                    c0 = fg * 1024 + pr * 256
                    dma('sp', wu[wb][:], wupb_d[:, c0:c0 + 256].rearrange("(k p) n -> p k n", p=128),
                        ['wupb'], [('wu', wb)], ('wu', wb))
                    for c in range(2):
                        b = ub % 2
                        ub += 1
                        for kc in range(16):
                            mm(pbank[b][:, :], wu[wb][:, kc, c * 128:(c + 1) * 128],
                               x1T[:, kc, tq * 512:(tq + 1) * 512], kc == 0, kc == 15,
                               [('wu', wb)] + [('mixT', 4 * tq + s) for s in range(4)], [('pb', b)])
                        act(rl[b][:], pbank[b][:, :], AF.Relu, [('pb', b)], [('rl', b)])
                        tt('dve', hT[hb][:, pr * 2 + c, :], rl[b][:], rl[b][:], ALU.mult, [('rl', b)],
                           [('hT', hb, pr * 2 + c)])
                for fo in range(4):
                    wb = wdi % 3
                    wdi += 1
                    dma('sp', wd[wb][:],
                        wdnb_d[fg * 1024:(fg + 1) * 1024, fo * 512:(fo + 1) * 512].rearrange("(c p) n -> p c n", p=128),
                        ['wdnb'], [('wd', wb)], ('wd', wb))
                    for sub in range(4):
                        b = 2 + db % 4
                        db += 1
                        for c in range(8):
                            mm(pbank[b][:, :], hT[hb][:, c, sub * 128:(sub + 1) * 128], wd[wb][:, c, :],
                               c == 0, c == 7, [('hT', hb, c), ('wd', wb)], [('pb', b)])
                        tt('dve', acc[:, sub, fo * 512:(fo + 1) * 512], acc[:, sub, fo * 512:(fo + 1) * 512],
                           pbank[b][:, :], ALU.add, [('pb', b), ('acc', sub)], [('acc', sub)])
            for sub in range(4):
                t0 = tq * 512 + sub * 128
                layer_norm(acc[:, sub, :], [('acc', sub)], lng2[:], lnb2[:], 'lng2', 'lnb2', stats2[sub % 2],
                           mv2[sub % 2], ('lnD', sub % 2), 'dve')
                dma('sp', out_d[t0:t0 + 128, :], acc[:, sub, :], [('acc', sub)], [('out', tq, sub)], ('outd', sub))
        S.add('sp', None, [('out', tq, sub) for tq in range(4) for sub in range(4)], [])
        if stop_after is not None:
            del S.ops[marks[stop_after]:]
            S.dma_since = [i for i, o in enumerate(S.ops) if o['dkey'] is not None]
            S.last_on = {}
            for i, o in enumerate(S.ops):
                if o['dkey'] is None and o['fn'] is not None:
                    S.last_on[o['eng']] = i
            S.barrier()
        S.emit(nc, st)
    return nc


def _t5_bucket(n):
    n = np.maximum(n, 0)
    max_exact = 16
    nf = np.maximum(n, 1).astype(np.float32)
    large = max_exact + (np.log(nf / max_exact) / math.log(128 / max_exact) * (32 - max_exact)).astype(np.int32)
    large = np.minimum(large, 31)
    return np.where(n < max_exact, n, large)


def prep_shared(inp):
    f = np.float32
    sh = {}
    sh["w_in"] = np.ascontiguousarray(inp["w_in"][0], dtype=f)
    sh["w_o"] = np.ascontiguousarray(inp["w_o"][0], dtype=f)
    sh["w_up"] = np.ascontiguousarray(inp["w_up"][0], dtype=f)
    sh["w_down"] = np.ascontiguousarray(inp["w_down"][0], dtype=f)
    cw = np.asarray(inp["conv_w"], dtype=f)[0, :, 0, :]
    sh["convw"] = np.ascontiguousarray(cw.reshape(4, 24, 128).transpose(2, 1, 0).reshape(128, 96))
    sh["alog_r"] = np.ascontiguousarray(np.broadcast_to(np.tile(np.asarray(inp["a_log"], f)[0], 16), (128, 128)))
    sh["dtb_r"] = np.ascontiguousarray(np.broadcast_to(np.tile(np.asarray(inp["dt_bias"], f)[0], 16), (128, 128)))
    sh["normw"] = np.ascontiguousarray(np.asarray(inp["delta_norm_w"], f)[0].reshape(128, 1))
    sh["sinks_r"] = np.ascontiguousarray(np.broadcast_to(np.asarray(inp["attn_sinks"], f)[0], (128, 16)))
    rb = np.asarray(inp["rel_bias"], f)
    k = np.arange(128)[:, None, None]
    kb = np.arange(2)[None, :, None]
    q = np.arange(128)[None, None, :]
    dist = q + 128 - (kb * 128 + k)
    valid = (dist >= 0) & (dist < 128)
    bidx = _t5_bucket(dist)
    tab = rb[bidx]
    tab = np.where(valid[..., None], tab, f(-1e30)).astype(f)
    sh["biasT"] = np.ascontiguousarray(tab.transpose(0, 3, 1, 2).reshape(128, 16 * 256))
    lnp = np.stack([np.asarray(inp[n], f)[0] for n in ("ln1_g", "ln1_b", "ln2_g", "ln2_b")])
    sh["lnp"] = np.ascontiguousarray(np.broadcast_to(lnp[:, None, :], (4, 128, D)))
    ii = np.arange(128)
    ident = np.eye(128, dtype=f)
    U = (ii[:, None] <= ii[None, :]).astype(f)
    Ls = (ii[None, :] < ii[:, None]).astype(f)
    Uns = (ii[None, :] >= ii[:, None]).astype(f)
    ones = np.ones((128, 128), f)
    BD = (ii[:, None] // 32 == ii[None, :] // 32).astype(f)
    O1 = ((ii[:, None] // 64 == ii[None, :] // 64).astype(f) - BD)
    O2 = (ii[:, None] // 64 != ii[None, :] // 64).astype(f)
    sh["cst"] = np.ascontiguousarray(np.concatenate([ident, U, Ls, Uns, ones, BD, O1, O2], axis=1))
    return sh


_CACHE = {}


def kernel(**inputs):
    x = np.asarray(inputs["x"], dtype=np.float32)
    sh = prep_shared(inputs)
    in_maps = []
    for b in range(8):
        m = dict(sh)
        m["x"] = np.ascontiguousarray(x[b])
        m["xT"] = np.ascontiguousarray(x[b].T)
        in_maps.append(m)
    if "nc" not in _CACHE:
        _CACHE["nc"] = build_program()
    res = run_bass_kernel_spmd(_CACHE["nc"], in_maps, core_ids=list(range(8)))
    return np.stack([np.asarray(r["out"], dtype=np.float32) for r in res.results], axis=0)
```
